# Optimizing a Trainium2 kernel written in Bass

```python
import jax, jax.numpy as jnp
from jax import lax
import numpy as np

D_MODEL = 2048
BATCH = 8
SEQ = 4096
DEPTH = 2
DEC_BATCH = 4
DEC_SEQ = 2048
PAST_LEN = 128

N_MIXERS = 2
N_DN_LAYERS = (DEPTH + N_MIXERS - 1) // N_MIXERS
N_FN_LAYERS = DEPTH // N_MIXERS
D_FF = 5632
DN_HEAD_K = 128
DN_HEAD_V = 128
DN_K_HEADS = D_MODEL // 128
DN_V_HEADS = 2 * DN_K_HEADS
DN_KEY_DIM = DN_K_HEADS * DN_HEAD_K
DN_VALUE_DIM = DN_V_HEADS * DN_HEAD_V
DN_QKV_DIM = 2 * DN_KEY_DIM + DN_VALUE_DIM
DN_GATE_COLS = 4 * DN_V_HEADS
DN_PROJ_DIM = DN_QKV_DIM + DN_VALUE_DIM + DN_GATE_COLS
CONV_WIDTH = 5
CONV_PAD = (CONV_WIDTH - 1) // 2
CHUNK = 64
FN_GROUPS = 8
FN_GROUP_DIM = D_MODEL // FN_GROUPS
EPS = 1e-6

kernel_name = "hybrid_deltanet_fnet_macaron_encoder"


def rms_norm(x, gain):
    xf = x.astype(jnp.float32)
    y = xf * lax.rsqrt(jnp.mean(xf * xf, axis=-1, keepdims=True) + EPS) * gain.astype(jnp.float32)
    return y.astype(x.dtype)


def swiglu(h, w_gate, w_up, w_down):
    return (jax.nn.silu(h @ w_gate) * (h @ w_up)) @ w_down


def l2_normalize(x):
    return x * lax.rsqrt(jnp.sum(x * x, axis=-1, keepdims=True) + EPS)


def centred_depthwise_conv(x, w):
    c = x.shape[-1]
    return lax.conv_general_dilated(
        x, w[:, None, :].astype(x.dtype), window_strides=(1,),
        padding=[(CONV_PAD, CONV_PAD)], dimension_numbers=("NWC", "WIO", "NWC"),
        feature_group_count=c)


def _to_chunks(t, n_chunks):
    b, _, h = t.shape[:3]
    t = t.reshape((b, n_chunks, CHUNK, h) + t.shape[3:])
    return jnp.moveaxis(jnp.moveaxis(t, 1, 0), 2, 3)


def chunk_gated_delta_rule(q, k, v, g, beta):
    b, t, h, dk = q.shape
    dv = v.shape[-1]
    n = t // CHUNK
    qc, kc, vc = _to_chunks(q, n), _to_chunks(k, n), _to_chunks(v, n)
    gc = jnp.cumsum(_to_chunks(g, n), axis=-1)
    bc = _to_chunks(beta, n)
    tril = jnp.tril(jnp.ones((CHUNK, CHUNK), dtype=bool))
    strict = jnp.tril(jnp.ones((CHUNK, CHUNK), dtype=bool), k=-1)
    diff = gc[..., :, None] - gc[..., None, :]
    decay = jnp.where(tril, jnp.exp(jnp.where(tril, diff, 0.0)), 0.0)
    kb = kc * bc[..., None]
    lower = jnp.where(strict, jnp.einsum("nbhid,nbhjd->nbhij", kb, kc) * decay, 0.0)
    eye = jnp.eye(CHUNK, dtype=q.dtype)
    tmat = lax.linalg.triangular_solve(eye + lower, jnp.broadcast_to(eye, lower.shape),
                                       left_side=True, lower=True, unit_diagonal=True)
    u = jnp.einsum("nbhij,nbhjd->nbhid", tmat, vc * bc[..., None])
    w = jnp.einsum("nbhij,nbhjd->nbhid", tmat, kb * jnp.exp(gc)[..., None])
    qk = jnp.where(tril, jnp.einsum("nbhid,nbhjd->nbhij", qc, kc) * decay, 0.0)

    def step(state, xs):
        q_i, k_i, u_i, w_i, qk_i, g_i = xs
        v_new = u_i - jnp.einsum("bhck,bhkv->bhcv", w_i, state)
        o_i = (jnp.einsum("bhck,bhkv->bhcv", q_i * jnp.exp(g_i)[..., None], state)
               + jnp.einsum("bhij,bhjv->bhiv", qk_i, v_new))
        g_last = g_i[..., -1:]
        state = (state * jnp.exp(g_last)[..., None]
                 + jnp.einsum("bhck,bhcv->bhkv", k_i * jnp.exp(g_last - g_i)[..., None], v_new))
        return state, o_i

    s0 = jnp.zeros((b, h, dk, dv), q.dtype)
    _, o = lax.scan(step, s0, (qc, kc, u, w, qk, gc))
    return jnp.transpose(o, (1, 0, 3, 2, 4)).reshape(b, t, h, dv)


def gated_deltanet(h, w_in, conv_w, a_log, dt_bias, out_norm, w_out):
    b, s, _ = h.shape
    proj = h @ w_in
    qkv = proj[..., :DN_QKV_DIM]
    z = proj[..., DN_QKV_DIM:DN_QKV_DIM + DN_VALUE_DIM]
    gates = proj[..., DN_QKV_DIM + DN_VALUE_DIM:].astype(jnp.float32).reshape(b, s, 2, 2, DN_V_HEADS)
    qkv = jax.nn.silu(centred_depthwise_conv(qkv, conv_w)).astype(jnp.float32)
    q = qkv[..., :DN_KEY_DIM].reshape(b, s, DN_K_HEADS, DN_HEAD_K)
    k = qkv[..., DN_KEY_DIM:2 * DN_KEY_DIM].reshape(b, s, DN_K_HEADS, DN_HEAD_K)
    v = qkv[..., 2 * DN_KEY_DIM:].reshape(b, s, DN_V_HEADS, DN_HEAD_V)
    rep = DN_V_HEADS // DN_K_HEADS
    q = jnp.repeat(l2_normalize(q) * (DN_HEAD_K ** -0.5), rep, axis=2)
    k = jnp.repeat(l2_normalize(k), rep, axis=2)
    g = -jnp.exp(a_log.astype(jnp.float32)) * jax.nn.softplus(gates[..., 0, :] + dt_bias.astype(jnp.float32))
    beta = jax.nn.sigmoid(gates[..., 1, :])
    o_fwd = chunk_gated_delta_rule(q, k, v, g[:, :, 0], beta[:, :, 0])
    flip = lambda t: jnp.flip(t, axis=1)
    o_bwd = flip(chunk_gated_delta_rule(flip(q), flip(k), flip(v), flip(g[:, :, 1]), flip(beta[:, :, 1])))
    o = o_fwd + o_bwd
    zf = z.astype(jnp.float32).reshape(b, s, DN_V_HEADS, DN_HEAD_V)
    o = (o * lax.rsqrt(jnp.mean(o * o, axis=-1, keepdims=True) + EPS)
         * out_norm.astype(jnp.float32) * jax.nn.silu(zf))
    return o.reshape(b, s, DN_VALUE_DIM).astype(h.dtype) @ w_out


def fourier_mixer(h, w_out):
    b, s, d = h.shape
    hg = h.astype(jnp.float32).reshape(b, s, FN_GROUPS, FN_GROUP_DIM)
    mixed = jnp.fft.fft2(hg, axes=(1, 3), norm="ortho").real
    return mixed.reshape(b, s, d).astype(h.dtype) @ w_out


def setup_inputs(seed: int = 0) -> dict:
    key = jax.random.key(seed)
    ks = iter(jax.random.split(key, 32))
    nrm = lambda shape, fan_in: jax.random.normal(next(ks), shape, jnp.float32) * (fan_in ** -0.5)
    gain = lambda shape: 1.0 + 0.02 * jax.random.normal(next(ks), shape, jnp.float32)
    x_prompt = jax.random.normal(next(ks), (BATCH, SEQ, D_MODEL), jnp.float32)
    x_sample = jax.random.normal(next(ks), (DEC_BATCH, DEC_SEQ, D_MODEL), jnp.float32)
    ffn1_norm = gain((DEPTH, D_MODEL))
    ffn1_w_gate = nrm((DEPTH, D_MODEL, D_FF), D_MODEL)
    ffn1_w_up = nrm((DEPTH, D_MODEL, D_FF), D_MODEL)
    ffn1_w_down = nrm((DEPTH, D_FF, D_MODEL), D_FF)
    mix_norm = gain((DEPTH, D_MODEL))
    dn_w_in = nrm((N_DN_LAYERS, D_MODEL, DN_PROJ_DIM), D_MODEL)
    dn_conv_w = nrm((N_DN_LAYERS, CONV_WIDTH, DN_QKV_DIM), CONV_WIDTH)
    dn_a_log = jnp.log(jax.random.uniform(next(ks), (N_DN_LAYERS, 2, DN_V_HEADS), jnp.float32, 1.0, 16.0))
    dt = jnp.exp(jax.random.uniform(next(ks), (N_DN_LAYERS, 2, DN_V_HEADS), jnp.float32,
                                    float(np.log(1e-3)), float(np.log(1e-1))))
    dn_dt_bias = dt + jnp.log(-jnp.expm1(-dt))
    dn_out_norm = gain((N_DN_LAYERS, DN_HEAD_V))
    dn_w_out = nrm((N_DN_LAYERS, DN_VALUE_DIM, D_MODEL), DN_VALUE_DIM)
    fn_w_out = nrm((N_FN_LAYERS, D_MODEL, D_MODEL), D_MODEL)
    ffn2_norm = gain((DEPTH, D_MODEL))
    ffn2_w_gate = nrm((DEPTH, D_MODEL, D_FF), D_MODEL)
    ffn2_w_up = nrm((DEPTH, D_MODEL, D_FF), D_MODEL)
    ffn2_w_down = nrm((DEPTH, D_FF, D_MODEL), D_FF)
    final_norm = gain((D_MODEL,))
    return {"x_prompt": x_prompt, "x_sample": x_sample,
            "ffn1_norm": ffn1_norm, "ffn1_w_gate": ffn1_w_gate, "ffn1_w_up": ffn1_w_up, "ffn1_w_down": ffn1_w_down,
            "mix_norm": mix_norm, "dn_w_in": dn_w_in, "dn_conv_w": dn_conv_w, "dn_a_log": dn_a_log,
            "dn_dt_bias": dn_dt_bias, "dn_out_norm": dn_out_norm, "dn_w_out": dn_w_out, "fn_w_out": fn_w_out,
            "ffn2_norm": ffn2_norm, "ffn2_w_gate": ffn2_w_gate, "ffn2_w_up": ffn2_w_up, "ffn2_w_down": ffn2_w_down,
            "final_norm": final_norm}


def reference(x_prompt, x_sample, ffn1_norm, ffn1_w_gate, ffn1_w_up, ffn1_w_down, mix_norm,
              dn_w_in, dn_conv_w, dn_a_log, dn_dt_bias, dn_out_norm, dn_w_out, fn_w_out,
              ffn2_norm, ffn2_w_gate, ffn2_w_up, ffn2_w_down, final_norm):
    def trunk(x):
        for i in range(DEPTH):
            x = x + 0.5 * swiglu(rms_norm(x, ffn1_norm[i]), ffn1_w_gate[i], ffn1_w_up[i], ffn1_w_down[i])
            h = rms_norm(x, mix_norm[i])
            j = i // N_MIXERS
            if i % N_MIXERS == 0:
                x = x + gated_deltanet(h, dn_w_in[j], dn_conv_w[j], dn_a_log[j], dn_dt_bias[j],
                                       dn_out_norm[j], dn_w_out[j])
            else:
                x = x + fourier_mixer(h, fn_w_out[j])
            x = x + 0.5 * swiglu(rms_norm(x, ffn2_norm[i]), ffn2_w_gate[i], ffn2_w_up[i], ffn2_w_down[i])
        return rms_norm(x, final_norm)

    y_prompt = trunk(x_prompt)
    y_sample = trunk(x_sample)
    return (y_prompt, y_sample)
```

```python
from contextlib import ExitStack

import numpy as np
import ml_dtypes
import concourse.bass as bass
import concourse.mybir as mybir
from concourse.bass_utils import run_bass_kernel_spmd

F32 = mybir.dt.float32
BF16 = mybir.dt.bfloat16
AF = mybir.ActivationFunctionType
ALU = mybir.AluOpType

D = 2048
DFF = 5632
NKC = D // 128
NFC = DFF // 128
T = 512
NSUB = T // 128
EPS = 1e-6
NSLOT = 6
DN_COLS = 12288
CH = 64


class Buf:
    __slots__ = ("name", "wtok", "rtoks", "dsem", "dcnt", "excl")

    def __init__(self, name="", excl=False):
        self.name = name
        self.excl = excl
        self.wtok = None
        self.rtoks = {}
        self.dsem = None
        self.dcnt = 0

    def addr(self, tok):
        k = id(tok[0])
        o = self.rtoks.get(k)
        if o is None or o[1] < tok[1]:
            self.rtoks[k] = tok


class Eng:
    def __init__(self, name, eng, sem):
        self.name = name
        self.e = eng
        self.sem = sem
        self.cnt = 0
        self.seen = {}

    def wait(self, tok):
        if tok is None:
            return
        sem, val = tok
        if sem is self.sem and self.name == "pe":
            return
        key = id(sem)
        if self.seen.get(key, 0) >= val:
            return
        self.e.wait_ge(sem, val)
        self.seen[key] = val


class FW:
    def __init__(self, nc, stack):
        self.nc = nc
        self.stack = stack
        mk = lambda n: stack.enter_context(nc.semaphore(n))
        self.pe = Eng("pe", nc.tensor, mk("s_pe"))
        self.act = Eng("act", nc.scalar, mk("s_act"))
        self.dve = Eng("dve", nc.vector, mk("s_dve"))
        self.pool = Eng("pool", nc.gpsimd, mk("s_pool"))
        self.sp = Eng("sp", nc.sync, mk("s_sp"))
        self.engs = [self.pe, self.act, self.dve, self.pool, self.sp]
        self.free_dsems = []
        self.pending = False
        self.live_dsems = []
        self.all_dsems = []
        self.nds = 0

    def sem(self, name):
        return self.stack.enter_context(self.nc.semaphore(name))

    def deps(self, eng, reads, writes):
        for b in reads:
            eng.wait(b.wtok)
            if b.excl:
                for t in list(b.rtoks.values()):
                    if t[0] is not eng.sem:
                        eng.wait(t)
        for b in writes:
            eng.wait(b.wtok)
            for t in list(b.rtoks.values()):
                eng.wait(t)

    def op(self, eng, fn, reads=(), writes=(), inc=True):
        self.deps(eng, reads, writes)
        ins = fn()
        if inc:
            ins.then_inc(eng.sem, 1)
            eng.cnt += 1
            tok = (eng.sem, eng.cnt)
            if eng is self.pe:
                self.pending = False
        else:
            tok = (eng.sem, eng.cnt + 1)
            self.pending = True
        for b in reads:
            b.addr(tok)
        for b in writes:
            b.wtok = tok
            b.rtoks = {}
        return tok

    def dma(self, q, out, in_, reads=(), writes=(), dbuf=None, **kw):
        self.deps(q, reads, writes)
        if dbuf is None:
            dbuf = writes[0] if writes else reads[0]
        if dbuf.dsem is None:
            if self.free_dsems:
                dbuf.dsem = self.free_dsems.pop()
            else:
                self.nds += 1
                dbuf.dsem = [self.sem(f"dsem{self.nds}"), 0]
            self.live_dsems.append(dbuf.dsem)
        ds = dbuf.dsem
        if dbuf.dcnt:
            q.wait((ds[0], ds[1] * 16))
        ins = q.e.dma_start(out=out, in_=in_, **kw)
        ins.then_inc(ds[0], 16)
        ds[1] += 1
        dbuf.dcnt += 1
        tok = (ds[0], ds[1] * 16)
        for b in reads:
            b.addr(tok)
        for b in writes:
            b.wtok = tok
            b.rtoks = {}
        return tok

    def barrier(self):
        toks = []
        for e in self.engs:
            if e.cnt:
                toks.append((e.sem, e.cnt))
        for ds in self.live_dsems:
            if ds[1]:
                toks.append((ds[0], ds[1] * 16))
        for e in self.engs:
            for t in toks:
                e.wait(t)

    def release_dsems(self, keep=()):
        keep_ids = {id(b.dsem) for b in keep if b.dsem is not None}
        nl = []
        for ds in self.live_dsems:
            if id(ds) in keep_ids:
                nl.append(ds)
            else:
                self.free_dsems.append(ds)
        self.live_dsems = nl


import os
G = 4
QSCALE = 128.0 ** -0.5


def build(seqs, dbg=None):
    NT = sum(seqs)
    soff = [sum(seqs[:i]) for i in range(len(seqs))]
    nc = bass.Bass("TRN2", target_bir_lowering=False)
    din = lambda n, sh, dt=F32: nc.dram_tensor(n, sh, dt, kind="ExternalInput").ap()
    dsc = lambda n, sh, dt=F32: nc.dram_tensor(n, sh, dt, kind="Internal").ap()
    x_d = din("x", [NT, D])
    wg1 = din("ffn1_w_gate", [2, D, DFF]); wu1 = din("ffn1_w_up", [2, D, DFF]); wd1 = din("ffn1_w_down", [2, DFF, D])
    wg2 = din("ffn2_w_gate", [2, D, DFF]); wu2 = din("ffn2_w_up", [2, D, DFF]); wd2 = din("ffn2_w_down", [2, DFF, D])
    win_d = din("dn_w_in", [1, D, 12416]); wout_d = din("dn_w_out", [1, 4096, D]); wfn_d = din("fn_w_out", [1, D, D])
    gains_d = din("gainsT", [128, 6 * NKC])
    gfin_d = din("gfin_bc", [128, D])
    ident_d = din("ident", [128, 128], BF16)
    convw_d = din("convwT", [128, 64 * 5])
    alog_d = din("alog_bc", [128, 64]); dtb_d = din("dtb_bc", [128, 64]); onorm_d = din("onorm", [128, 1])
    cmask_d = din("cmask", [64, 6 * 64])
    cs256_d = din("cs256", [128, 2 * 512], BF16)
    tabs_d = [din(f"dft{S}", [S // 512, S // 512, 128, 4096], BF16) for S in sorted(set(seqs))]
    tab_of = {S: tabs_d[i] for i, S in enumerate(sorted(set(seqs)))}
    y_d = nc.dram_tensor("y", [NT, D], F32, kind="ExternalOutput").ap()

    FFN_SL = 66
    wscr_f = [dsc(f"wscr{f}", [FFN_SL, 128, 4096], BF16) for f in range(4)]
    wscr_in = dsc("wscr_in", [49, 128, 4096], BF16)
    wscr_out = dsc("wscr_out", [16, 128, 4096], BF16)
    wscr_fn = dsc("wscr_fn", [8, 128, 4096], BF16)
    xres = dsc("xres", [NT, D])
    projqkv = dsc("projqkv", [8192, NT])
    projz = dsc("projz", [4096, NT])
    gproc = dsc("gproc", [NT, 128])
    ogT = dsc("ogT", [4096, NT], BF16)
    ycs = dsc("ycs", [NT, 8 * 512], BF16)
    mixT = dsc("mixT", [D, NT], BF16)

    with ExitStack() as st:
        fw = FW(nc, st)
        op = fw.op
        pe, act, dve, pool, sp = fw.pe, fw.act, fw.dve, fw.pool, fw.sp
        uid = [0]

        def mksb(stack):
            def f(n, sh, dt=F32):
                uid[0] += 1
                return stack.enter_context(nc.sbuf_tensor(f"sb{uid[0]}_{n}", sh, dt))
            return f
        sb = mksb(st)

        const_bufs = []

        def cload(name, shape, src, dt=F32):
            t = sb(name, shape, dt)
            b = Buf(name)
            const_bufs.append(b)
            fw.dma(pool, t[:], src, writes=[b])
            return t, b
        gains, b_gains = cload("gains", [128, 6 * NKC], gains_d[:, :])
        gfin, b_gfin = cload("gfin", [128, D], gfin_d[:, :])
        ident, b_ident = cload("ident", [128, 128], ident_d[:, :], BF16)
        convw, b_convw = cload("convw", [128, 64 * 5], convw_d[:, :])
        nega, b_nega = cload("nega", [128, 64], alog_d[:, :])
        dtb, b_dtb = cload("dtb", [128, 64], dtb_d[:, :])
        onorm, b_onorm = cload("onorm", [128, 1], onorm_d[:, :])
        cmask, b_cmask = cload("cmask", [64, 6 * 64], cmask_d[:, :])
        cs256, b_cs256 = cload("cs256", [128, 2 * 512], cs256_d[:, :], BF16)
        ones32 = sb("ones32", [64, 128]); b_ones32 = Buf("ones32")
        onesbf = sb("onesbf", [128, 128], BF16); b_onesbf = Buf("onesbf")
        op(dve, lambda: nc.vector.memset(ones32[:], 1.0), writes=[b_ones32])
        op(dve, lambda: nc.vector.memset(onesbf[:], 1.0), writes=[b_onesbf])
        op(act, lambda: nc.scalar.activation(out=nega[:], in_=nega[:], func=AF.Exp), reads=[b_nega], writes=[b_nega])
        op(dve, lambda: nc.vector.tensor_scalar_mul(out=nega[:], in0=nega[:], scalar1=-1.0), reads=[b_nega], writes=[b_nega])
        cm = lambda i: cmask[:, i * 64:(i + 1) * 64]

        banks = [st.enter_context(nc.psum_tensor(f"pb{i}", [128, 512], F32)) for i in range(8)]
        b_bk = [Buf(f"bk{i}", excl=True) for i in range(8)]
        bk_n = [0]

        bkd_n = [0, 0]

        def nbank_d(d):
            i = 2 * d + bkd_n[d] % 2
            bkd_n[d] += 1
            return banks[i], b_bk[i]

        def nbank(lo=0, hi=8):
            i = lo + bk_n[0] % (hi - lo)
            bk_n[0] += 1
            return banks[i], b_bk[i]

        cvb = [Buf(f"cv{i}") for i in range(8)]
        cvn = [0]

        def conv(dst, src):
            b = cvb[cvn[0] % len(cvb)]
            cvn[0] += 1
            fw.dma(pool, dst, src, dbuf=b)

        def ffn_slab(f, kind, i):
            return wscr_f[f][kind * 22 + i]

        def conv_fm(dst_slab, w2d, c0, ncols=256):
            conv(dst_slab[:, 0:NKC * ncols].rearrange("p (kc c) -> p kc c", kc=NKC),
                 w2d[:, c0:c0 + ncols].rearrange("(kc p) c -> p kc c", p=128))

        def conv_tm(dst_slab, w2d, r0, n0):
            conv(dst_slab.rearrange("p (c n) -> p c n", c=4),
                 w2d[r0:r0 + 512, n0:n0 + 1024].rearrange("(c p) n -> p c n", p=128))

        ffn_w = [(wg1, wu1, wd1, 0), (wg2, wu2, wd2, 0), (wg1, wu1, wd1, 1), (wg2, wu2, wd2, 1)]
        run_ffn = dbg in (None, "ffn")
        run_dn = dbg in (None, "dnmix")
        run_fn = dbg in (None, "fnmix")
        def conv_ffn(f):
            wg, wu, wd, l = ffn_w[f]
            for s in range(22):
                conv_fm(ffn_slab(f, 0, s), wg[l], s * 256)
                yield
                conv_fm(ffn_slab(f, 1, s), wu[l], s * 256)
                yield
            for h in range(2):
                for s in range(11):
                    conv_tm(ffn_slab(f, 2, h * 11 + s), wd[l], s * 512, h * 1024)
                    yield

        def conv_late():
            for f in (1, 2, 3):
                yield from conv_ffn(f)
            for h in range(2):
                for s in range(8):
                    conv_tm(wscr_out[h * 8 + s], wout_d[0], s * 512, h * 1024)
                    yield
            for h in range(2):
                for s in range(4):
                    conv_tm(wscr_fn[h * 4 + s], wfn_d[0], s * 512, h * 1024)
                    yield
        late_gen = None
        if dbg is None:
            for _ in conv_ffn(0):
                pass
            for s in range(48):
                conv_fm(wscr_in[s], win_d[0], s * 256)
            conv_fm(wscr_in[48], win_d[0], 12288, 128)
            late_gen = conv_late()
        else:
            if run_ffn:
                for f in range(4):
                    for _ in conv_ffn(f):
                        pass
            if run_dn:
                for s in range(48):
                    conv_fm(wscr_in[s], win_d[0], s * 256)
                conv_fm(wscr_in[48], win_d[0], 12288, 128)
                for h in range(2):
                    for s in range(8):
                        conv_tm(wscr_out[h * 8 + s], wout_d[0], s * 512, h * 1024)
            if run_fn:
                for h in range(2):
                    for s in range(4):
                        conv_tm(wscr_fn[h * 4 + s], wfn_d[0], s * 512, h * 1024)
        fw.barrier()
        fw.release_dsems(keep=cvb)

        class Row:
            pass

        def row_ctx(ph):
            R = Row()
            sbp = mksb(ph)
            xt = sbp("xt", [128, NSUB, D]); b_x = Buf("x")
            hn = [sbp(f"hn{i}", [128, D], BF16) for i in range(2)]; b_hn = [Buf(f"hn{i}") for i in range(2)]
            junk = sbp("junk", [128, D], BF16); b_junk = Buf("junk")
            stat = sbp("stat", [128, 16]); b_stat = Buf("stat")
            hT = sbp("hT", [128, NKC, T], BF16); b_hT = Buf("hT")
            aT = sbp("aT", [128, NFC, T], BF16); b_aT = Buf("aT")
            sg = [sbp(f"sg{i}", [128, T]) for i in range(2)]; b_sg = [Buf(f"sg{i}") for i in range(2)]
            stg = [sbp(f"stg{i}", [128, 2, T]) for i in range(2)]; b_stg = [Buf(f"stg{i}") for i in range(2)]
            NSL = 7
            wring = [sbp(f"wr{i}", [128, 4096], BF16) for i in range(NSL)]; b_wr = [Buf(f"wr{i}") for i in range(NSL)]
            wr_n = [0]
            R.xt, R.b_x, R.hT, R.b_hT, R.aT, R.b_aT = xt, b_x, hT, b_hT, aT, b_aT

            def wload(slab_ap):
                i = wr_n[0] % NSL
                wr_n[0] += 1
                fw.dma(sp, wring[i][:], slab_ap, writes=[b_wr[i]])
                return wring[i], b_wr[i]
            R.wload = wload

            def rstd_of(j):
                op(dve, lambda: nc.vector.memset(stat[:, j:j + 1], 0.0), writes=[b_stat])
                op(act, lambda: nc.scalar.activation(out=junk[:], in_=xt[:, j, :], func=AF.Square, accum_out=stat[:, j:j + 1]),
                   reads=[b_x], writes=[b_junk, b_stat])
                op(dve, lambda: nc.vector.tensor_scalar(out=stat[:, 4 + j:5 + j], in0=stat[:, j:j + 1], scalar1=1.0 / D, scalar2=EPS,
                                                         op0=ALU.mult, op1=ALU.add), reads=[b_stat], writes=[b_stat])
                op(act, lambda: nc.scalar.activation(out=stat[:, 8 + j:9 + j], in_=stat[:, 4 + j:5 + j], func=AF.Sqrt),
                   reads=[b_stat], writes=[b_stat])
                op(dve, lambda: nc.vector.reciprocal(out=stat[:, 12 + j:13 + j], in_=stat[:, 8 + j:9 + j]), reads=[b_stat], writes=[b_stat])

            def norm_transpose(gi):
                for j in range(NSUB):
                    h, bh = hn[j % 2], b_hn[j % 2]
                    rstd_of(j)
                    op(act, lambda: nc.scalar.activation(out=h[:], in_=xt[:, j, :], func=AF.Copy, scale=stat[:, 12 + j:13 + j]),
                       reads=[b_x, b_stat], writes=[bh])
                    for q in range(4):
                        bank, bb = nbank()
                        for c in range(4):
                            kc = q * 4 + c
                            op(pe, lambda: nc.tensor.matmul(bank[:, c * 128:(c + 1) * 128], lhsT=h[:, kc * 128:(kc + 1) * 128], rhs=ident[:],
                                                            start=True, stop=True), reads=[bh, b_ident], writes=[bb], inc=(c == 3))
                        op(dve, lambda: nc.vector.tensor_tensor(
                            out=hT[:, q * 4:(q + 1) * 4, j * 128:(j + 1) * 128],
                            in0=bank[:].rearrange("p (c t) -> p c t", c=4),
                            in1=gains[:, gi * NKC + q * 4: gi * NKC + q * 4 + 4].unsqueeze(2).to_broadcast([128, 4, 128]),
                            op=ALU.mult), reads=[bb, b_gains], writes=[b_hT])
            R.norm_transpose = norm_transpose

            def tm_matmul(lhs_fn, lhs_bufs, nk, slab_fn, scale):
                for half in range(2):
                    accs = [[(banks[j * 2 + c], b_bk[j * 2 + c]) for c in range(2)] for j in range(NSUB)]
                    for s in range(nk // 4):
                        slot, bs = wload(slab_fn(half, s))
                        for c4 in range(4):
                            kc = 4 * s + c4
                            for j in range(NSUB):
                                for c in range(2):
                                    bank, bb = accs[j][c]
                                    op(pe, lambda: nc.tensor.matmul(bank[:], lhsT=lhs_fn(kc, j),
                                                                    rhs=slot[:, c4 * 1024 + c * 512: c4 * 1024 + (c + 1) * 512],
                                                                    start=(kc == 0), stop=(kc == nk - 1)),
                                       reads=[bs] + lhs_bufs, writes=[bb],
                                       inc=(kc == nk - 1) or (c4 == 3 and j == NSUB - 1 and c == 1))
                    for j in range(NSUB):
                        for c in range(2):
                            bank, bb = accs[j][c]
                            col = half * 1024 + c * 512
                            op(dve, lambda: nc.vector.scalar_tensor_tensor(out=xt[:, j, col:col + 512], in0=bank[:], scalar=float(scale),
                                                                           in1=xt[:, j, col:col + 512], op0=ALU.mult, op1=ALU.add),
                               reads=[bb, b_x], writes=[b_x])
            R.tm_matmul = tm_matmul

            def ffn(f, gi):
                norm_transpose(gi)
                for s in range(22):
                    slg, bsg_ = wload(ffn_slab(f, 0, s))
                    slu, bsu_ = wload(ffn_slab(f, 1, s))
                    for c2 in range(2):
                        fc = 2 * s + c2
                        bg, bbg = nbank()
                        bu, bbu = nbank()
                        for (bank, bb, sl, bsl) in ((bg, bbg, slg, bsg_), (bu, bbu, slu, bsu_)):
                            for kc in range(NKC):
                                op(pe, lambda: nc.tensor.matmul(bank[:], lhsT=sl[:, kc * 256 + c2 * 128: kc * 256 + (c2 + 1) * 128],
                                                                rhs=hT[:, kc, :], start=(kc == 0), stop=(kc == NKC - 1)),
                                   reads=[bsl, b_hT], writes=[bb], inc=(kc == NKC - 1))
                        sgt, bsgt = sg[fc % 2], b_sg[fc % 2]
                        op(act, lambda: nc.scalar.activation(out=sgt[:], in_=bg[:], func=AF.Silu), reads=[bbg], writes=[bsgt])
                        op(dve, lambda: nc.vector.tensor_tensor(out=aT[:, fc, :], in0=sgt[:], in1=bu[:], op=ALU.mult),
                           reads=[bsgt, bbu], writes=[b_aT])
                tm_matmul(lambda kc, j: aT[:, kc, j * 128:(j + 1) * 128], [b_aT], NFC,
                          lambda half, s: ffn_slab(f, 2, half * 11 + s), 0.5)
            R.ffn = ffn

            def load_x(src, t0):
                fw.dma(pool, xt[:], src[t0:t0 + T, :].rearrange("(j p) d -> p j d", p=128), writes=[b_x])

            def store_x(dst, t0):
                fw.dma(pool, dst[t0:t0 + T, :].rearrange("(j p) d -> p j d", p=128), xt[:], reads=[b_x])
            R.load_x, R.store_x = load_x, store_x

            def final_norm():
                for j in range(NSUB):
                    rstd_of(j)
                    op(dve, lambda: nc.vector.scalar_tensor_tensor(out=xt[:, j, :], in0=xt[:, j, :], scalar=stat[:, 12 + j:13 + j],
                                                                   in1=gfin[:], op0=ALU.mult, op1=ALU.mult),
                       reads=[b_x, b_stat, b_gfin], writes=[b_x])
            R.final_norm = final_norm

            gst = sbp("gst", [128, NSUB, 128]); b_gst = Buf("gst")
            gtmp = sbp("gtmp", [128, 64]); b_gtmp = Buf("gtmp")

            def dn_inproj(t0):
                for s in range(48):
                    slot, bs = wload(wscr_in[s])
                    sgb, bsgb = stg[s % 2], b_stg[s % 2]
                    for c2 in range(2):
                        bank, bb = nbank()
                        for kc in range(NKC):
                            op(pe, lambda: nc.tensor.matmul(bank[:], lhsT=slot[:, kc * 256 + c2 * 128: kc * 256 + (c2 + 1) * 128],
                                                            rhs=hT[:, kc, :], start=(kc == 0), stop=(kc == NKC - 1)),
                               reads=[bs, b_hT], writes=[bb], inc=(kc == NKC - 1))
                        if c2 == 0:
                            op(act, lambda: nc.scalar.copy(out=sgb[:, c2, :], in_=bank[:]), reads=[bb], writes=[bsgb])
                        else:
                            op(dve, lambda: nc.vector.tensor_copy(out=sgb[:, c2, :], in_=bank[:]), reads=[bb], writes=[bsgb])
                    if s < 32:
                        dst = projqkv[s * 256:(s + 1) * 256, t0:t0 + T]
                    else:
                        dst = projz[(s - 32) * 256:(s - 31) * 256, t0:t0 + T]
                    fw.dma(pool, dst.rearrange("(c p) t -> p c t", p=128), sgb[:], reads=[bsgb])
                    if late_gen is not None and s % 2 == 1:
                        next(late_gen, None)
                slot, bs = wload(wscr_in[48])
                for j in range(NSUB):
                    bank, bb = nbank()
                    for kc in range(NKC):
                        op(pe, lambda: nc.tensor.matmul(bank[:, 0:128], lhsT=hT[:, kc, j * 128:(j + 1) * 128],
                                                        rhs=slot[:, kc * 128:(kc + 1) * 128], start=(kc == 0), stop=(kc == NKC - 1)),
                           reads=[bs, b_hT], writes=[bb], inc=(kc == NKC - 1))
                    ps4 = bank[:, 0:128].rearrange("p (d k h) -> p d k h", d=2, k=2)
                    g4 = gst[:, j, :].rearrange("p (d k h) -> p d k h", d=2, k=2)
                    gt3 = gtmp[:].rearrange("p (d h) -> p d h", d=2)
                    op(dve, lambda: nc.vector.tensor_tensor(out=gt3, in0=ps4[:, :, 0, :], in1=dtb[:].rearrange("p (d h) -> p d h", d=2),
                                                            op=ALU.add), reads=[bb, b_dtb], writes=[b_gtmp])
                    op(act, lambda: nc.scalar.activation(out=gtmp[:], in_=gtmp[:], func=AF.Exp), reads=[b_gtmp], writes=[b_gtmp])
                    op(dve, lambda: nc.vector.tensor_scalar_add(out=gtmp[:], in0=gtmp[:], scalar1=1.0), reads=[b_gtmp], writes=[b_gtmp])
                    op(act, lambda: nc.scalar.activation(out=gtmp[:], in_=gtmp[:], func=AF.Ln), reads=[b_gtmp], writes=[b_gtmp])
                    op(dve, lambda: nc.vector.tensor_tensor(out=g4[:, :, 0, :], in0=gt3, in1=nega[:].rearrange("p (d h) -> p d h", d=2),
                                                            op=ALU.mult), reads=[b_gtmp, b_nega], writes=[b_gst])
                    op(act, lambda: nc.scalar.activation(out=g4[:, :, 1, :], in_=ps4[:, :, 1, :], func=AF.Sigmoid), reads=[bb], writes=[b_gst])
                fw.dma(pool, gproc[t0:t0 + T, :].rearrange("(j p) c -> p j c", p=128), gst[:], reads=[b_gst])
            R.dn_inproj = dn_inproj

            def dn_outproj(t0):
                fw.dma(pool, aT[:, 0:32, :], ogT[:, t0:t0 + T].rearrange("(kc p) t -> p kc t", p=128), writes=[b_aT])
                tm_matmul(lambda kc, j: aT[:, kc, j * 128:(j + 1) * 128], [b_aT], 32, lambda half, s: wscr_out[half * 8 + s], 1.0)
            R.dn_outproj = dn_outproj

            yst = [sbp(f"yst{i}", [128, 2, 512], BF16) for i in range(2)]; b_yst = [Buf(f"yst{i}") for i in range(2)]

            def fn_step1(t0):
                n = 0
                for j in range(NSUB):
                    for g2 in range(4):
                        ys, bys = yst[n % 2], b_yst[n % 2]
                        n += 1
                        for gg in range(2):
                            g = g2 * 2 + gg
                            bank, bb = nbank()
                            for k2 in range(2):
                                op(pe, lambda: nc.tensor.matmul(bank[:], lhsT=hT[:, 2 * g + k2, j * 128:(j + 1) * 128],
                                                                rhs=cs256[:, k2 * 512:(k2 + 1) * 512], start=(k2 == 0), stop=(k2 == 1)),
                                   reads=[b_hT, b_cs256], writes=[bb], inc=(k2 == 1))
                            if gg == 0:
                                op(act, lambda: nc.scalar.copy(out=ys[:, gg, :], in_=bank[:]), reads=[bb], writes=[bys])
                            else:
                                op(dve, lambda: nc.vector.tensor_copy(out=ys[:, gg, :], in_=bank[:]), reads=[bb], writes=[bys])
                        fw.dma(pool, ycs[t0 + j * 128:t0 + (j + 1) * 128, g2 * 1024:(g2 + 1) * 1024].rearrange("p (a b) -> p a b", a=2),
                               ys[:], reads=[bys])
            R.fn_step1 = fn_step1

            def fn_outproj(t0):
                fw.dma(pool, hT[:], mixT[:, t0:t0 + T].rearrange("(kc p) t -> p kc t", p=128), writes=[b_hT])
                tm_matmul(lambda kc, j: hT[:, kc, j * 128:(j + 1) * 128], [b_hT], NKC, lambda half, s: wscr_fn[half * 4 + s], 1.0)
            R.fn_outproj = fn_outproj
            return R

        def dn_core(S, s0):
            NCH = S // CH
            NG = NCH // G
            CB = 512
            OFFS = 0
            with ExitStack() as ph:
                sbp = mksb(ph)
                qT = sbp("qT", [128, S], BF16); b_qT = Buf("qT")
                kT = sbp("kT", [128, S], BF16); b_kT = Buf("kT")
                Ktok = sbp("Ktok", [64, NCH, 128], BF16); b_Ktok = Buf("Ktok")
                Vtok = sbp("Vtok", [64, NCH, 2, 128], BF16); b_Vtok = Buf("Vtok")
                oT = sbp("oT", [128, 2 * S]); b_oT = Buf("oT")
                gall = oT[0:64, :].rearrange("p (n c) -> p n c", c=128)
                g8 = sbp("g8", [64, 4, NCH, 2]); b_g8 = Buf("g8")
                class PS:
                    pass
                psets = []
                for i in range(2):
                    p_ = PS()
                    p_.raw = sbp(f"raw{i}", [128, CB + 4]); p_.b_raw = Buf(f"raw{i}")
                    p_.acc = sbp(f"acc{i}", [128, CB]); p_.b_acc = Buf(f"acc{i}")
                    p_.sil = sbp(f"sil{i}", [128, CB]); p_.b_sil = Buf(f"sil{i}")
                    p_.silb = sbp(f"silb{i}", [128, CB], BF16); p_.b_silb = Buf(f"silb{i}")
                    p_.rn = sbp(f"rn{i}", [128, 512]); p_.b_rn = Buf(f"rn{i}")
                    p_.lo, p_.hi = 4 * i, 4 * i + 4
                    psets.append(p_)
                Sst = sbp("Sst", [128, 4, 128]); b_Sst = Buf("Sst")
                Sbf = sbp("Sbf", [128, 4, 128], BF16); b_Sbf = Buf("Sbf")
                vn = sbp("vn", [64, 4, 128], BF16); b_vnall = Buf("vnall")
                ogs = [sbp(f"ogs{i}", [128, 512], BF16) for i in range(2)]; b_ogs = [Buf(f"ogs{i}") for i in range(2)]
                class BA:
                    pass
                bas = [[None, None], [None, None]]
                for d, par_ in ((0, 0), (0, 1), (1, 0), (1, 1)):
                    a = BA()
                    dn_ = f"{d}{par_}"
                    a.TT = sbp(f"TT{dn_}", [64, 8, 64], BF16); a.b_TT = Buf(f"TT{dn_}")
                    a.vb = sbp(f"vb{dn_}", [64, G, 2, 128], BF16); a.b_vb = Buf(f"vb{dn_}")
                    a.kgl = sbp(f"kgl{dn_}", [64, G, 2, 128], BF16); a.b_kgl = Buf(f"kgl{dn_}")
                    a.nwT = sbp(f"nwT{dn_}", [128, 8, 64], BF16); a.b_nwT = Buf(f"nwT{dn_}")
                    a.qgT = sbp(f"qgT{dn_}", [128, 8, 64], BF16); a.b_qgT = Buf(f"qgT{dn_}")
                    a.qkT = sbp(f"qkT{dn_}", [64, 8, 64], BF16); a.b_qkT = Buf(f"qkT{dn_}")
                    a.egl = sbp(f"egl{dn_}", [128, 8]); a.b_egl = Buf(f"egl{dn_}")
                    bas[d][par_] = a
                eg4 = [sbp(f"eg4{i}", [128, G, 4]) for i in range(2)]; b_eg4 = [Buf(f"eg4{i}") for i in range(2)]
                class TMP:
                    pass
                tmps = []
                for d in range(2):
                    t = TMP()
                    t.Mt = sbp(f"Mt{d}", [64, 512]); t.b_Mt = Buf(f"Mt{d}")
                    t.Dt, t.b_Dt, t.Ld, t.b_Ld = t.Mt, t.b_Mt, t.Mt, t.b_Mt
                    t.dec = sbp(f"dec{d}", [64, 512]); t.b_dec = Buf(f"dec{d}")
                    t.As = sbp(f"As{d}", [64, G, 64]); t.b_As = Buf(f"As{d}")
                    t.Pb = [sbp(f"Pb{d}{i}", [64, 8, 64], BF16) for i in range(2)]; t.b_Pb = [Buf(f"Pb{d}{i}") for i in range(2)]
                    t.Nb = [sbp(f"Nb{d}{i}", [64, 8, 64], BF16) for i in range(2)]; t.b_Nb = [Buf(f"Nb{d}{i}") for i in range(2)]
                    t.Rb = [sbp(f"Rb{d}{i}", [64, 8, 64], BF16) for i in range(2)]; t.b_Rb = [Buf(f"Rb{d}{i}") for i in range(2)]
                    t.qk = sbp(f"qk{d}", [64, 8, 64], BF16); t.b_qk = Buf(f"qk{d}")
                    t.kbg = sbp(f"kbg{d}", [64, G, 2, 128], BF16); t.b_kbg = Buf(f"kbg{d}")
                    t.egr = sbp(f"egr{d}", [128, 512]); t.b_egr = Buf(f"egr{d}")
                    t.sm = sbp(f"sm{d}", [64, 6, 8]); t.b_sm = Buf(f"sm{d}")
                    t.qks = sbp(f"qks{d}", [64, G, 64]); t.b_qks = Buf(f"qks{d}")
                    t.Asb = sbp(f"Asb{d}", [64, G, 2, 64]); t.b_Asb = Buf(f"Asb{d}")
                    tmps.append(t)
                v4 = lambda t: t[:].rearrange("p (n v c) -> p n v c", n=G, v=2) if len(t.shape) == 2 else t[:].rearrange("p (n v) c -> p n v c", v=2)

                def conv_silu(P, rows_ap, cc, cb, out_t, b_out):
                    t0 = cb * CB
                    r, br, acc, b_acc = P.raw, P.b_raw, P.acc, P.b_acc
                    lo, hi = max(t0 - 2, 0), min(t0 + CB + 2, S)
                    if t0 == 0:
                        op(dve, lambda: nc.vector.memset(r[:, 0:2], 0.0), writes=[br])
                    if t0 + CB == S:
                        op(dve, lambda: nc.vector.memset(r[:, CB + 2:CB + 4], 0.0), writes=[br])
                    fw.dma(pool, r[:, lo - (t0 - 2):hi - (t0 - 2)], rows_ap[:, s0 + lo:s0 + hi], writes=[br])
                    yield
                    w = lambda tap: convw[:, cc * 5 + tap:cc * 5 + tap + 1]
                    op(act, lambda: nc.scalar.activation(out=acc[:], in_=r[:, 0:CB], func=AF.Copy, scale=w(0)),
                       reads=[br, b_convw], writes=[b_acc])
                    yield
                    for tap in range(1, 5):
                        op(dve, lambda: nc.vector.scalar_tensor_tensor(out=acc[:], in0=r[:, tap:tap + CB], scalar=w(tap), in1=acc[:],
                                                                       op0=ALU.mult, op1=ALU.add), reads=[br, b_convw, b_acc], writes=[b_acc])
                        yield
                    op(act, lambda: nc.scalar.activation(out=out_t[:], in_=acc[:], func=AF.Silu), reads=[b_acc], writes=[b_out])
                    yield

                def l2norm_to(P, dst, b_dst, cb, scale):
                    sil, b_sil, silb, b_silb, rn, b_rn = P.sil, P.b_sil, P.silb, P.b_silb, P.rn, P.b_rn
                    op(dve, lambda: nc.vector.tensor_tensor(out=silb[:], in0=sil[:], in1=sil[:], op=ALU.mult), reads=[b_sil], writes=[b_silb])
                    yield
                    for h in range(CB // 512):
                        bank, bb = nbank(P.lo, P.hi)
                        op(pe, lambda: nc.tensor.matmul(bank[:], lhsT=onesbf[:], rhs=silb[:, h * 512:(h + 1) * 512], start=True, stop=True),
                           reads=[b_onesbf, b_silb], writes=[bb])
                        yield
                        op(dve, lambda: nc.vector.tensor_scalar_add(out=rn[:], in0=bank[:], scalar1=EPS), reads=[bb], writes=[b_rn])
                        yield
                        op(act, lambda: nc.scalar.activation(out=rn[:], in_=rn[:], func=AF.Ln), reads=[b_rn], writes=[b_rn])
                        yield
                        op(act, lambda: nc.scalar.activation(out=rn[:], in_=rn[:], func=AF.Exp, scale=-0.5), reads=[b_rn], writes=[b_rn])
                        yield
                        c0 = cb * CB + h * 512
                        op(dve, lambda: nc.vector.scalar_tensor_tensor(out=dst[:, c0:c0 + 512], in0=sil[:, h * 512:(h + 1) * 512], scalar=float(scale),
                                                                       in1=rn[:], op0=ALU.mult, op1=ALU.mult), reads=[b_sil, b_rn], writes=[b_dst])
                        yield

                def to_tok(P, srcT, b_src, c_off, dst_fn, b_dst, n0, nn):
                    for q in range(0, nn, 4):
                        bank, bb = nbank(P.lo, P.hi)
                        for c in range(4):
                            col = c_off + (q + c) * 64
                            op(pe, lambda: nc.tensor.matmul(bank[0:64, c * 128:(c + 1) * 128], lhsT=srcT[:, col:col + 64], rhs=ident[:],
                                                            start=True, stop=True), reads=[b_src, b_ident], writes=[bb], inc=(c == 3))
                        yield
                        op(act, lambda: nc.scalar.copy(out=dst_fn(n0 + q), in_=bank[0:64, :].rearrange("p (c k) -> p c k", c=4)),
                           reads=[bb], writes=[b_dst])
                        yield

                def rr(gens):
                    gens = list(gens)
                    while gens:
                        for g_ in list(gens):
                            try:
                                next(g_)
                            except StopIteration:
                                gens.remove(g_)

                def chain_q(P, hk):
                    for cb in range(S // CB):
                        yield from conv_silu(P, projqkv[hk * 128:(hk + 1) * 128, :], hk, cb, P.sil, P.b_sil)
                        yield from l2norm_to(P, qT, b_qT, cb, QSCALE)

                def chain_k(P, hk):
                    nn = CB // CH
                    for cb in range(S // CB):
                        yield from conv_silu(P, projqkv[2048 + hk * 128:2048 + (hk + 1) * 128, :], 16 + hk, cb, P.sil, P.b_sil)
                        yield from l2norm_to(P, kT, b_kT, cb, 1.0)
                        yield from to_tok(P, kT, b_kT, cb * CB, lambda n: Ktok[:, n:n + 4, :], b_Ktok, cb * nn, nn)

                def chain_v(P, hk, v):
                    nn = CB // CH
                    vh = 2 * hk + v
                    for cb in range(S // CB):
                        yield from conv_silu(P, projqkv[4096 + vh * 128:4096 + (vh + 1) * 128, :], 32 + vh, cb, P.silb, P.b_silb)
                        yield from to_tok(P, P.silb, P.b_silb, 0, lambda n: Vtok[:, n:n + 4, v, :], b_Vtok, cb * nn, nn)

                def chain_fin(P, hk, v):
                    vh = 2 * hk + v
                    r, br, silb, b_silb, rn, b_rn = P.raw, P.b_raw, P.silb, P.b_silb, P.rn, P.b_rn
                    for h in range(S // 512):
                        osl = oT[:, v * S + h * 512:v * S + (h + 1) * 512]
                        fw.dma(pool, r[:, 0:512], projz[vh * 128:(vh + 1) * 128, s0 + h * 512:s0 + (h + 1) * 512], writes=[br])
                        yield
                        op(dve, lambda: nc.vector.tensor_tensor(out=silb[:, 0:512], in0=osl, in1=osl, op=ALU.mult), reads=[b_oT], writes=[b_silb])
                        yield
                        bank, bb = nbank(P.lo, P.hi)
                        op(pe, lambda: nc.tensor.matmul(bank[:], lhsT=onesbf[:], rhs=silb[:, 0:512], start=True, stop=True),
                           reads=[b_onesbf, b_silb], writes=[bb])
                        yield
                        op(dve, lambda: nc.vector.tensor_scalar(out=rn[:], in0=bank[:], scalar1=1.0 / 128, scalar2=EPS, op0=ALU.mult, op1=ALU.add),
                           reads=[bb], writes=[b_rn])
                        yield
                        op(act, lambda: nc.scalar.activation(out=rn[:], in_=rn[:], func=AF.Ln), reads=[b_rn], writes=[b_rn])
                        yield
                        op(act, lambda: nc.scalar.activation(out=rn[:], in_=rn[:], func=AF.Exp, scale=-0.5), reads=[b_rn], writes=[b_rn])
                        yield
                        op(act, lambda: nc.scalar.activation(out=r[:, 0:512], in_=r[:, 0:512], func=AF.Silu), reads=[br], writes=[br])
                        yield
                        op(dve, lambda: nc.vector.tensor_tensor(out=rn[:], in0=rn[:], in1=osl, op=ALU.mult), reads=[b_rn, b_oT], writes=[b_rn])
                        yield
                        og, bog = ogs[v], b_ogs[v]
                        op(dve, lambda: nc.vector.scalar_tensor_tensor(out=og[:], in0=rn[:], scalar=onorm[:, 0:1], in1=r[:, 0:512],
                                                                       op0=ALU.mult, op1=ALU.mult), reads=[b_rn, b_onorm, br], writes=[bog])
                        yield
                        fw.dma(pool, ogT[vh * 128:(vh + 1) * 128, s0 + h * 512:s0 + (h + 1) * 512], og[:], reads=[bog])
                        yield

                def precompute(hk, d, cg, par):
                    a = bas[d][par]
                    tm_ = tmps[d]
                    Mt, b_Mt, Dt, b_Dt, Ld, b_Ld, dec, b_dec, As, b_As = tm_.Mt, tm_.b_Mt, tm_.Dt, tm_.b_Dt, tm_.Ld, tm_.b_Ld, tm_.dec, tm_.b_dec, tm_.As, tm_.b_As
                    Pb, b_Pb, Nb, b_Nb, Rb, b_Rb, qk, b_qk = tm_.Pb, tm_.b_Pb, tm_.Nb, tm_.b_Nb, tm_.Rb, tm_.b_Rb, tm_.qk, tm_.b_qk
                    kbg, b_kbg, egr, b_egr, sm, b_sm = tm_.kbg, tm_.b_kbg, tm_.egr, tm_.b_egr, tm_.sm, tm_.b_sm
                    n0 = cg * G
                    c0 = n0 * CH
                    gv = g8[:, d, n0:n0 + G, :]
                    bv = g8[:, 2 + d, n0:n0 + G, :]
                    gv2 = gv.rearrange("p n v -> p (n v)")
                    smv = lambda i: sm[:, i, :].rearrange("p (n v) -> p n v", v=2)
                    bA, bbA = nbank_d(d)
                    bQ, bbQ = nbank_d(d)
                    for n in range(G):
                        cs = slice(c0 + n * CH, c0 + (n + 1) * CH)
                        op(pe, lambda: nc.tensor.matmul(bA[0:64, n * 64:(n + 1) * 64], lhsT=kT[:, cs], rhs=kT[:, cs], start=True, stop=True),
                           reads=[b_kT], writes=[bbA], inc=(n == G - 1))
                        if not fw.pending:
                            yield
                    for n in range(G):
                        cs = slice(c0 + n * CH, c0 + (n + 1) * CH)
                        op(pe, lambda: nc.tensor.matmul(bQ[0:64, n * 64:(n + 1) * 64], lhsT=qT[:, cs], rhs=kT[:, cs], start=True, stop=True),
                           reads=[b_qT, b_kT], writes=[bbQ], inc=(n == G - 1))
                        if not fw.pending:
                            yield
                    op(dve, lambda: nc.vector.tensor_tensor(out=As[:], in0=bA[0:64, 0:G * 64].rearrange("p (n c) -> p n c", n=G),
                                                            in1=cm(4 + d).unsqueeze(1).to_broadcast([64, G, 64]), op=ALU.mult),
                       reads=[bbA, b_cmask], writes=[b_As])
                    if not fw.pending:
                        yield
                    op(dve, lambda: nc.vector.tensor_tensor(out=tm_.Asb[:], in0=As[:].unsqueeze(2).to_broadcast([64, G, 2, 64]),
                                                              in1=bv.unsqueeze(3).to_broadcast([64, G, 2, 64]), op=ALU.mult),
                       reads=[b_As, b_g8], writes=[tm_.b_Asb])
                    if not fw.pending:
                        yield
                    op(act, lambda: nc.scalar.copy(out=tm_.qks[:].rearrange("p n c -> p (n c)"), in_=bQ[0:64, 0:G * 64]), reads=[bbQ], writes=[tm_.b_qks])
                    if not fw.pending:
                        yield
                    bG, bbG = nbank_d(d)
                    op(pe, lambda: nc.tensor.matmul(bG[0:64, 0:8], lhsT=cm(d), rhs=gv2, start=True, stop=True),
                       reads=[b_cmask, b_g8], writes=[bbG], inc=False)
                    if not fw.pending:
                        yield
                    op(pe, lambda: nc.tensor.matmul(bG[:, 8:16], lhsT=ones32[:], rhs=gv2, start=True, stop=True),
                       reads=[b_ones32, b_g8], writes=[bbG])
                    if not fw.pending:
                        yield
                    op(dve, lambda: nc.vector.tensor_tensor(out=v4(Mt), in0=cm(d).unsqueeze(1).unsqueeze(1).to_broadcast([64, G, 2, 64]),
                                                            in1=gv.unsqueeze(3).to_broadcast([64, G, 2, 64]), op=ALU.mult),
                       reads=[b_cmask, b_g8], writes=[b_Mt])
                    if not fw.pending:
                        yield
                    bR, bbR = nbank_d(d)
                    op(pe, lambda: nc.tensor.matmul(bR[:], lhsT=ones32[:], rhs=Mt[:], start=True, stop=True),
                       reads=[b_ones32, b_Mt], writes=[bbR])
                    if not fw.pending:
                        yield
                    op(act, lambda: nc.scalar.copy(out=sm[:, 0, :], in_=bG[0:64, 0:8]), reads=[bbG], writes=[b_sm])
                    if not fw.pending:
                        yield
                    if d == 0:
                        op(act, lambda: nc.scalar.activation(out=eg4[par][:, :, 0:2], in_=bG[:, 8:16].rearrange("p (n v) -> p n v", v=2), func=AF.Exp),
                           reads=[bbG], writes=[b_eg4[par]])
                    else:
                        for nl_ in range(G):
                            op(act, lambda: nc.scalar.activation(out=eg4[par][:, G - 1 - nl_, 2:4], in_=bG[:, 8 + 2 * nl_:10 + 2 * nl_], func=AF.Exp),
                               reads=[bbG], writes=[b_eg4[par]])
                    if not fw.pending:
                        yield
                    op(act, lambda: nc.scalar.copy(out=sm[:, 5, :], in_=bG[0:64, 8:16]), reads=[bbG], writes=[b_sm])
                    if not fw.pending:
                        yield
                    op(dve, lambda: nc.vector.tensor_tensor(out=sm[:, 4, :], in0=sm[:, 5, :], in1=sm[:, 0, :], op=ALU.subtract),
                       reads=[b_sm], writes=[b_sm])
                    if not fw.pending:
                        yield
                    op(act, lambda: nc.scalar.activation(out=sm[:, 3, :], in_=sm[:, 4, :], func=AF.Exp), reads=[b_sm], writes=[b_sm])
                    if not fw.pending:
                        yield
                    op(act, lambda: nc.scalar.activation(out=sm[:, 1, :], in_=sm[:, 0, :], func=AF.Exp), reads=[b_sm], writes=[b_sm])
                    if not fw.pending:
                        yield
                    op(dve, lambda: nc.vector.tensor_tensor(out=smv(2), in0=smv(1), in1=bv, op=ALU.mult), reads=[b_sm, b_g8], writes=[b_sm])
                    if not fw.pending:
                        yield
                    op(dve, lambda: nc.vector.tensor_tensor(out=v4(Dt), in0=smv(0).unsqueeze(3).to_broadcast([64, G, 2, 64]),
                                                            in1=bR[0:64, :].rearrange("p (n v c) -> p n v c", n=G, v=2), op=ALU.subtract),
                       reads=[b_sm, bbR], writes=[b_Dt])
                    if not fw.pending:
                        yield
                    op(dve, lambda: nc.vector.tensor_tensor(out=v4(Dt), in0=v4(Dt), in1=cm(2 + d).unsqueeze(1).unsqueeze(1).to_broadcast([64, G, 2, 64]),
                                                            op=ALU.add), reads=[b_Dt, b_cmask], writes=[b_Dt])
                    if not fw.pending:
                        yield
                    op(act, lambda: nc.scalar.activation(out=dec[:], in_=Dt[:], func=AF.Exp), reads=[b_Dt], writes=[b_dec])
                    if not fw.pending:
                        yield
                    op(act, lambda: nc.scalar.activation(out=egr[:], in_=bR[:], func=AF.Exp), reads=[bbR], writes=[b_egr])
                    if not fw.pending:
                        yield
                    op(dve, lambda: nc.vector.tensor_tensor(out=v4(Pb[0]), in0=v4(dec), in1=tm_.Asb[:], op=ALU.mult),
                       reads=[b_dec, tm_.b_Asb], writes=[b_Pb[0]])
                    if not fw.pending:
                        yield
                    op(dve, lambda: nc.vector.tensor_tensor(out=v4(qk), in0=v4(dec),
                                                            in1=tm_.qks[:].unsqueeze(2).to_broadcast([64, G, 2, 64]),
                                                            op=ALU.mult), reads=[b_dec, tm_.b_qks], writes=[b_qk])
                    if not fw.pending:
                        yield
                    for (src, bsrc, dst, bdst) in ((Pb[0], b_Pb[0], Nb[0], b_Nb[0]), (qk, b_qk, a.qkT, a.b_qkT)):
                        bank, bb = nbank_d(d)
                        for c in range(8):
                            op(pe, lambda: nc.tensor.matmul(bank[0:64, c * 64:(c + 1) * 64], lhsT=src[:, c, :], rhs=ident[0:64, 0:64],
                                                            start=True, stop=True), reads=[bsrc, b_ident], writes=[bb], inc=(c == 7))
                            if not fw.pending:
                                yield
                        op(act, lambda: nc.scalar.copy(out=dst[:].rearrange("p a b -> p (a b)"), in_=bank[0:64, :]), reads=[bb], writes=[bdst])
                        if not fw.pending:
                            yield
                        if src is Pb[0]:
                            op(dve, lambda: nc.vector.tensor_tensor(out=Rb[0][:], in0=ident[0:64, 0:64].unsqueeze(1).to_broadcast([64, 8, 64]),
                                                                    in1=bank[0:64, :].rearrange("p (a b) -> p a b", a=8), op=ALU.subtract),
                               reads=[bb, b_ident], writes=[b_Rb[0]])
                            if not fw.pending:
                                yield
                    for k in range(5):
                        Pc, bPc, Pn, bPn = Pb[k % 2], b_Pb[k % 2], Pb[(k + 1) % 2], b_Pb[(k + 1) % 2]
                        Ncur, bNc, Nn, bNn = Nb[k % 2], b_Nb[k % 2], Nb[(k + 1) % 2], b_Nb[(k + 1) % 2]
                        Rc, bRc = Rb[k % 2], b_Rb[k % 2]
                        last = (k == 4)
                        Rn, bRn = (a.TT, a.b_TT) if last else (Rb[(k + 1) % 2], b_Rb[(k + 1) % 2])
                        bank, bb = nbank_d(d)
                        for c in range(8):
                            op(pe, lambda: nc.tensor.matmul(bank[0:64, c * 64:(c + 1) * 64], lhsT=Ncur[:, c, :], rhs=Pc[:, c, :], start=True, stop=True),
                               reads=[bNc, bPc], writes=[bb], inc=(c == 7))
                            if not fw.pending:
                                yield
                        op(act, lambda: nc.scalar.copy(out=Pn[:].rearrange("p a b -> p (a b)"), in_=bank[0:64, :]), reads=[bb], writes=[bPn])
                        if not fw.pending:
                            yield
                        if not fw.pending:
                            yield
                        if not last:
                            bank2, bb2 = nbank_d(d)
                            for c in range(8):
                                op(pe, lambda: nc.tensor.matmul(bank2[0:64, c * 64:(c + 1) * 64], lhsT=Pc[:, c, :], rhs=Ncur[:, c, :], start=True, stop=True),
                                   reads=[bNc, bPc], writes=[bb2], inc=(c == 7))
                                if not fw.pending:
                                    yield
                            op(dve, lambda: nc.vector.tensor_copy(out=Nn[:].rearrange("p a b -> p (a b)"), in_=bank2[0:64, :]), reads=[bb2], writes=[bNn])
                            if not fw.pending:
                                yield
                        bank3, bb3 = nbank_d(d)
                        for c in range(8):
                            op(pe, lambda: nc.tensor.matmul(bank3[0:64, c * 64:(c + 1) * 64], lhsT=ident[0:64, 0:64], rhs=Rc[:, c, :], start=True, stop=False),
                               reads=[b_ident, bRc], writes=[bb3], inc=False)
                            op(pe, lambda: nc.tensor.matmul(bank3[0:64, c * 64:(c + 1) * 64], lhsT=Pn[:, c, :], rhs=Rc[:, c, :], start=False, stop=True),
                               reads=[bPn, bRc], writes=[bb3], inc=(c == 7))
                            if not fw.pending:
                                yield
                        if k % 2 == 0:
                            op(dve, lambda: nc.vector.tensor_copy(out=Rn[:].rearrange("p a b -> p (a b)"), in_=bank3[0:64, :]), reads=[bb3], writes=[bRn])
                        else:
                            op(act, lambda: nc.scalar.copy(out=Rn[:].rearrange("p a b -> p (a b)"), in_=bank3[0:64, :]), reads=[bb3], writes=[bRn])
                        if not fw.pending:
                            yield
                    op(dve, lambda: nc.vector.tensor_tensor(out=a.vb[:], in0=Vtok[:, n0:n0 + G, :, :], in1=bv.unsqueeze(3).to_broadcast([64, G, 2, 128]),
                                                            op=ALU.mult), reads=[b_Vtok, b_g8], writes=[a.b_vb])
                    if not fw.pending:
                        yield
                    op(dve, lambda: nc.vector.tensor_tensor(out=kbg[:], in0=Ktok[:, n0:n0 + G, :].unsqueeze(2).to_broadcast([64, G, 2, 128]),
                                                            in1=smv(2).unsqueeze(3).to_broadcast([64, G, 2, 128]), op=ALU.mult),
                       reads=[b_Ktok, b_sm], writes=[b_kbg])
                    if not fw.pending:
                        yield
                    op(dve, lambda: nc.vector.tensor_tensor(out=a.kgl[:], in0=Ktok[:, n0:n0 + G, :].unsqueeze(2).to_broadcast([64, G, 2, 128]),
                                                            in1=smv(3).unsqueeze(3).to_broadcast([64, G, 2, 128]), op=ALU.mult),
                       reads=[b_Ktok, b_sm], writes=[a.b_kgl])
                    if not fw.pending:
                        yield
                    op(dve, lambda: nc.vector.tensor_tensor(out=a.qgT[:].rearrange("p (n v) c -> p n v c", v=2),
                                                            in0=qT[:, c0:c0 + G * CH].rearrange("p (n c) -> p n c", n=G).unsqueeze(2).to_broadcast([128, G, 2, 64]),
                                                            in1=egr[:].rearrange("p (n v c) -> p n v c", n=G, v=2), op=ALU.mult),
                       reads=[b_qT, b_egr], writes=[a.b_qgT])
                    if not fw.pending:
                        yield
                    bank, bb = nbank_d(d)
                    for c in range(8):
                        op(pe, lambda: nc.tensor.matmul(bank[:, c * 64:(c + 1) * 64], lhsT=kbg[:, c // 2, c % 2, :], rhs=a.TT[:, c, :], start=True, stop=True),
                           reads=[b_kbg, a.b_TT], writes=[bb], inc=(c == 7))
                        if not fw.pending:
                            yield
                    op(act, lambda: nc.scalar.mul(out=a.nwT[:].rearrange("p a b -> p (a b)"), in_=bank[:], mul=-1.0), reads=[bb], writes=[a.b_nwT])
                    if not fw.pending:
                        yield

                def scan(hk, cgs, par):
                    vps_ = lambda ch: banks[6][0:64, ch * 128:(ch + 1) * 128]
                    sps_ = lambda ch: banks[7][:, ch * 128:(ch + 1) * 128]
                    ops_f = lambda d, v, nl: banks[4 + d][:, (v * G + nl) * 64:(v * G + nl + 1) * 64]
                    for s in range(G):
                        chains = []
                        for d in range(2):
                            nl = s if d == 0 else G - 1 - s
                            for v in range(2):
                                chains.append((d, v, nl, d * 2 + v, nl * 2 + v, bas[d][par]))
                        for i_, (d, v, nl, ch, c, a) in enumerate(chains):
                            op(pe, lambda: nc.tensor.matmul(vps_(ch), lhsT=a.TT[:, c, :], rhs=a.vb[:, nl, v, :], start=True, stop=False),
                               reads=[a.b_TT, a.b_vb], writes=[b_bk[6]], inc=False)
                            op(pe, lambda: nc.tensor.matmul(vps_(ch), lhsT=a.nwT[:, c, :], rhs=Sbf[:, ch, :], start=False, stop=True),
                               reads=[a.b_nwT, b_Sbf], writes=[b_bk[6]], inc=(i_ == 3))
                        yield
                        op(act, lambda: nc.scalar.copy(out=vn[:].rearrange("p a b -> p (a b)"), in_=banks[6][0:64, :]), reads=[b_bk[6]], writes=[b_vnall])
                        yield
                        for i_, (d, v, nl, ch, c, a) in enumerate(chains):
                            op(pe, lambda: nc.tensor.matmul(ops_f(d, v, nl), lhsT=Sbf[:, ch, :], rhs=a.qgT[:, c, :], start=True, stop=False),
                               reads=[b_Sbf, a.b_qgT], writes=[b_bk[4 + d]], inc=False)
                            op(pe, lambda: nc.tensor.matmul(ops_f(d, v, nl), lhsT=vn[:, ch, :], rhs=a.qkT[:, c, :], start=False, stop=True),
                               reads=[b_vnall, a.b_qkT], writes=[b_bk[4 + d]], inc=False)
                            op(pe, lambda: nc.tensor.matmul(sps_(ch), lhsT=a.kgl[:, nl, v, :], rhs=vn[:, ch, :], start=True, stop=True),
                               reads=[a.b_kgl, b_vnall], writes=[b_bk[7]], inc=(i_ == 3))
                        yield
                        for (d, v, nl, ch, c, a) in chains:
                            pass
                        op(dve, lambda: nc.vector.tensor_tensor(out=Sst[:], in0=Sst[:], in1=eg4[par][:, s, :].unsqueeze(2).to_broadcast([128, 4, 128]), op=ALU.mult),
                           reads=[b_Sst, b_eg4[par]], writes=[b_Sst])
                        op(dve, lambda: nc.vector.tensor_tensor(out=Sst[:].rearrange("p a b -> p (a b)"), in0=banks[7][:, :], in1=Sst[:].rearrange("p a b -> p (a b)"), op=ALU.add),
                           reads=[b_Sst, b_bk[7]], writes=[b_Sst])
                        yield
                        op(act, lambda: nc.scalar.copy(out=Sbf[:].rearrange("p a b -> p (a b)"), in_=Sst[:].rearrange("p a b -> p (a b)")),
                           reads=[b_Sst], writes=[b_Sbf])
                        yield
                    for d in range(2):
                        c0 = cgs[d] * G * CH
                        for v in range(2):
                            dst = oT[:, v * S + c0:v * S + c0 + G * CH]
                            op(dve, lambda: nc.vector.tensor_tensor(out=dst, in0=banks[4 + d][:, v * G * 64:(v + 1) * G * 64], in1=dst, op=ALU.add),
                               reads=[b_bk[4 + d], b_oT], writes=[b_oT])

                for hk in range(16):
                    fw.dma(pool, gall, gproc[s0:s0 + S, :].rearrange("(n p) c -> p n c", p=64), writes=[b_oT])
                    for kind in range(2):
                        for d in range(2):
                            col = d * 64 + kind * 32 + 2 * hk
                            op(dve, lambda: nc.vector.tensor_copy(out=g8[:, kind * 2 + d, :, :], in_=gall[:, :, col:col + 2]), reads=[b_oT], writes=[b_g8])
                    op(dve, lambda: nc.vector.memset(oT[:], 0.0), writes=[b_oT])
                    op(dve, lambda: nc.vector.memset(Sst[:], 0.0), writes=[b_Sst])
                    op(dve, lambda: nc.vector.memset(Sbf[:], 0.0), writes=[b_Sbf])
                    rr([chain_q(psets[0], hk), chain_k(psets[1], hk)])
                    rr([chain_v(psets[0], hk, 0), chain_v(psets[1], hk, 1)])
                    DN_STOP = 9
                    HK_MAX = 16
                    if hk >= HK_MAX:
                        break
                    def pre_batch(sgi):
                        gens = [precompute(hk, d, [sgi, NG - 1 - sgi][d], sgi % 2) for d in range(2)]
                        for _o in range(OFFS):
                            try:
                                next(gens[0])
                                yield
                            except StopIteration:
                                gens.pop(0)
                                break
                        while gens:
                            for g_ in list(gens):
                                try:
                                    next(g_)
                                    yield
                                except StopIteration:
                                    gens.remove(g_)

                    def drain(g):
                        for _ in g:
                            pass
                    drain(pre_batch(0))
                    KPRE = 5
                    for sgi in range(NG):
                        nxt = pre_batch(sgi + 1) if sgi + 1 < NG else iter(())
                        for _ in scan(hk, [sgi, NG - 1 - sgi], sgi % 2):
                            for _k in range(KPRE):
                                next(nxt, None)
                        drain(nxt)
                    if DN_STOP < 4:
                        continue
                    rr([chain_fin(psets[0], hk, 0), chain_fin(psets[1], hk, 1)])
                fw.barrier()
                fw.release_dsems()

        def fn_core(S, s0):
            tab = tab_of[S]
            NSC = S // 128
            with ExitStack() as ph:
                sbp = mksb(ph)
                yres = sbp("yres", [128, NSC, 1024], BF16); b_yres = Buf("yres")
                NSL = 6
                ring = [sbp(f"fr{i}", [128, 4096], BF16) for i in range(NSL)]; b_ring = [Buf(f"fr{i}") for i in range(NSL)]
                mst = [sbp(f"mst{i}", [128, 4, 512], BF16) for i in range(2)]; b_mst = [Buf(f"mst{i}") for i in range(2)]
                rn_ = [0]
                for ps_ in range(4):
                    fw.dma(pool, yres[:], ycs[s0:s0 + S, ps_ * 1024:(ps_ + 1) * 1024].rearrange("(sc p) c -> p sc c", p=128), writes=[b_yres])
                    for kt in range(S // 512):
                        accs = [(banks[i], b_bk[i]) for i in range(4)] if kt % 2 == 0 else [(banks[4 + i], b_bk[4 + i]) for i in range(4)]
                        for sl in range(NSC // 4):
                            i = rn_[0] % NSL
                            rn_[0] += 1
                            fw.dma(sp, ring[i][:], tab[kt, sl], writes=[b_ring[i]])
                            slot, bs = ring[i], b_ring[i]
                            for s4 in range(4):
                                sc = sl * 4 + s4
                                for cc in range(4):
                                    gg, hf = cc // 2, cc % 2
                                    bank, bb = accs[cc]
                                    for t in range(2):
                                        first = (sc == 0 and t == 0)
                                        lastm = (sc == NSC - 1 and t == 1)
                                        op(pe, lambda: nc.tensor.matmul(bank[:], lhsT=yres[:, sc, gg * 512 + t * 256 + hf * 128: gg * 512 + t * 256 + (hf + 1) * 128],
                                                                        rhs=slot[:, (s4 * 2 + t) * 512:(s4 * 2 + t + 1) * 512], start=first, stop=lastm),
                                           reads=[b_yres, bs], writes=[bb], inc=lastm or (s4 == 3 and cc == 3 and t == 1))
                        ms, bms = mst[kt % 2], b_mst[kt % 2]
                        for cc in range(4):
                            bank, bb = accs[cc]
                            if cc % 2 == 0:
                                op(act, lambda: nc.scalar.copy(out=ms[:, cc, :], in_=bank[:]), reads=[bb], writes=[bms])
                            else:
                                op(dve, lambda: nc.vector.tensor_copy(out=ms[:, cc, :], in_=bank[:]), reads=[bb], writes=[bms])
                        fw.dma(pool, mixT[ps_ * 512:(ps_ + 1) * 512, s0 + kt * 512:s0 + (kt + 1) * 512].rearrange("(c p) t -> p c t", p=128),
                               ms[:], reads=[bms])
                fw.barrier()
                fw.release_dsems()

        def row_phase(body):
            with ExitStack() as ph:
                R = row_ctx(ph)
                for t0 in range(0, NT, T):
                    body(R, t0)
                fw.barrier()
                fw.release_dsems()

        if dbg == "ffn":
            def body(R, t0):
                R.load_x(x_d, t0); R.ffn(0, 0); R.ffn(1, 4); R.ffn(2, 1); R.ffn(3, 5); R.final_norm(); R.store_x(y_d, t0)
            row_phase(body)
        elif dbg == "dnmix":
            def bodyA(R, t0):
                R.load_x(x_d, t0); R.norm_transpose(2); R.dn_inproj(t0)
            row_phase(bodyA)
            for S, s0 in zip(seqs, soff):
                dn_core(S, s0)
            def bodyC(R, t0):
                R.load_x(x_d, t0); R.dn_outproj(t0); R.store_x(y_d, t0)
            row_phase(bodyC)
        elif dbg == "fnmix":
            def bodyA(R, t0):
                R.load_x(x_d, t0); R.norm_transpose(3); R.fn_step1(t0)
            row_phase(bodyA)
            for S, s0 in zip(seqs, soff):
                fn_core(S, s0)
            def bodyC(R, t0):
                R.load_x(x_d, t0); R.fn_outproj(t0); R.store_x(y_d, t0)
            row_phase(bodyC)
        else:
            def bodyA(R, t0):
                R.load_x(x_d, t0); R.ffn(0, 0); R.store_x(xres, t0); R.norm_transpose(2); R.dn_inproj(t0)
                if t0 + T >= NT:
                    for _ in late_gen:
                        pass
            row_phase(bodyA)
            for S, s0 in zip(seqs, soff):
                dn_core(S, s0)
            def bodyC(R, t0):
                R.load_x(xres, t0); R.dn_outproj(t0); R.ffn(1, 4); R.ffn(2, 1); R.store_x(xres, t0); R.norm_transpose(3); R.fn_step1(t0)
            row_phase(bodyC)
            for S, s0 in zip(seqs, soff):
                fn_core(S, s0)
            def bodyE(R, t0):
                R.load_x(xres, t0); R.fn_outproj(t0); R.ffn(3, 5); R.final_norm(); R.store_x(y_d, t0)
            row_phase(bodyE)
        fw.barrier()
    return nc


def _dft_table(S):
    s = np.arange(S, dtype=np.int64)
    ang = (np.outer(s, s) % S).astype(np.float64) * (2.0 * np.pi / S)
    sc = 1.0 / np.sqrt(S)
    tab = np.stack([np.cos(ang) * sc, -np.sin(ang) * sc], axis=0).astype(np.float32)
    tab = tab.reshape(2, S // 512, 4, 128, S // 512, 512)
    tab = np.transpose(tab, (4, 1, 3, 2, 0, 5))
    return np.ascontiguousarray(tab.reshape(S // 512, S // 512, 128, 4096)).astype(ml_dtypes.bfloat16)


def _host_consts(inp, seqs):
    gl = [inp["ffn1_norm"][0], inp["ffn1_norm"][1], inp["mix_norm"][0], inp["mix_norm"][1],
          inp["ffn2_norm"][0], inp["ffn2_norm"][1]]
    c = {}
    c["gainsT"] = np.concatenate([np.ascontiguousarray(g.reshape(NKC, 128).T) for g in gl], axis=1).astype(np.float32)
    c["gfin_bc"] = np.ascontiguousarray(np.broadcast_to(inp["final_norm"][None, :], (128, D))).astype(np.float32)
    c["ident"] = np.eye(128, dtype=np.float32).astype(ml_dtypes.bfloat16)
    cw = inp["dn_conv_w"][0]
    c["convwT"] = np.ascontiguousarray(cw.reshape(5, 64, 128).transpose(2, 1, 0).reshape(128, 320)).astype(np.float32)
    c["alog_bc"] = np.ascontiguousarray(np.broadcast_to(inp["dn_a_log"][0].reshape(1, 64), (128, 64))).astype(np.float32)
    c["dtb_bc"] = np.ascontiguousarray(np.broadcast_to(inp["dn_dt_bias"][0].reshape(1, 64), (128, 64))).astype(np.float32)
    c["onorm"] = np.ascontiguousarray(inp["dn_out_norm"][0].reshape(128, 1)).astype(np.float32)
    i = np.arange(64)[:, None]; j = np.arange(64)[None, :]
    tri0 = (i <= j).astype(np.float32); tri1 = (i >= j).astype(np.float32)
    neg0 = np.where(j <= i, 0.0, -30000.0).astype(np.float32); neg1 = np.where(j >= i, 0.0, -30000.0).astype(np.float32)
    st0 = (j < i).astype(np.float32); st1 = (j > i).astype(np.float32)
    c["cmask"] = np.ascontiguousarray(np.concatenate([tri0, tri1, neg0, neg1, st0, st1], axis=1))
    a = np.arange(256, dtype=np.float64)
    ang = np.outer(a, a) * (2 * np.pi / 256)
    cs = np.concatenate([np.cos(ang), np.sin(ang)], axis=1) / 16.0
    c["cs256"] = np.ascontiguousarray(cs.reshape(2, 128, 512).transpose(1, 0, 2).reshape(128, 1024)).astype(np.float32).astype(ml_dtypes.bfloat16)
    for S in sorted(set(seqs)):
        c[f"dft{S}"] = _dft_table(S)
    return c


W_NAMES = ["ffn1_w_gate", "ffn1_w_up", "ffn1_w_down", "ffn2_w_gate", "ffn2_w_up", "ffn2_w_down",
           "dn_w_in", "dn_w_out", "fn_w_out"]


def kernel(**inp):
    inp = {k: np.asarray(v) for k, v in inp.items()}
    xp, xs = inp["x_prompt"], inp["x_sample"]
    B, S, _ = xp.shape
    B2, S2, _ = xs.shape
    n = 8
    nc = build([S, S2])
    consts = _host_consts(inp, [S, S2])
    in_maps = []
    for c in range(n):
        m = {"x": np.ascontiguousarray(np.concatenate([xp[c], xs[c % B2]], axis=0))}
        for k in W_NAMES:
            m[k] = inp[k]
        m.update(consts)
        in_maps.append(m)
    res = run_bass_kernel_spmd(nc, in_maps, core_ids=list(range(n)))
    yp = np.stack([res.results[c]["y"][:S] for c in range(n)], axis=0)
    ys = np.stack([res.results[c]["y"][S:] for c in range(B2)], axis=0)
    return (yp.astype(np.float32), ys.astype(np.float32))
```

```python
from contextlib import ExitStack

import numpy as np
import ml_dtypes
import concourse.bass as bass
import concourse.mybir as mybir
from concourse.bass_utils import run_bass_kernel_spmd

F32 = mybir.dt.float32
BF16 = mybir.dt.bfloat16
AF = mybir.ActivationFunctionType
ALU = mybir.AluOpType

D = 2048
DFF = 5632
NKC = D // 128
NFC = DFF // 128
T = 512
NSUB = T // 128
EPS = 1e-6
NSLOT = 6
DN_COLS = 12288
CH = 64


class Buf:
    __slots__ = ("name", "wtok", "rtoks", "dsem", "dcnt", "excl")

    def __init__(self, name="", excl=False):
        self.name = name
        self.excl = excl
        self.wtok = None
        self.rtoks = {}
        self.dsem = None
        self.dcnt = 0

    def addr(self, tok):
        k = id(tok[0])
        o = self.rtoks.get(k)
        if o is None or o[1] < tok[1]:
            self.rtoks[k] = tok


class Eng:
    def __init__(self, name, eng, sem):
        self.name = name
        self.e = eng
        self.sem = sem
        self.cnt = 0
        self.seen = {}

    def wait(self, tok):
        if tok is None:
            return
        sem, val = tok
        if sem is self.sem and self.name == "pe":
            return
        key = id(sem)
        if self.seen.get(key, 0) >= val:
            return
        self.e.wait_ge(sem, val)
        self.seen[key] = val


class FW:
    def __init__(self, nc, stack):
        self.nc = nc
        self.stack = stack
        mk = lambda n: stack.enter_context(nc.semaphore(n))
        self.pe = Eng("pe", nc.tensor, mk("s_pe"))
        self.act = Eng("act", nc.scalar, mk("s_act"))
        self.dve = Eng("dve", nc.vector, mk("s_dve"))
        self.pool = Eng("pool", nc.gpsimd, mk("s_pool"))
        self.sp = Eng("sp", nc.sync, mk("s_sp"))
        self.engs = [self.pe, self.act, self.dve, self.pool, self.sp]
        self.free_dsems = []
        self.pending = False
        self.live_dsems = []
        self.all_dsems = []
        self.nds = 0

    def sem(self, name):
        return self.stack.enter_context(self.nc.semaphore(name))

    def deps(self, eng, reads, writes):
        for b in reads:
            eng.wait(b.wtok)
            if b.excl:
                for t in list(b.rtoks.values()):
                    if t[0] is not eng.sem:
                        eng.wait(t)
        for b in writes:
            eng.wait(b.wtok)
            for t in list(b.rtoks.values()):
                eng.wait(t)

    def op(self, eng, fn, reads=(), writes=(), inc=True):
        self.deps(eng, reads, writes)
        ins = fn()
        if inc:
            ins.then_inc(eng.sem, 1)
            eng.cnt += 1
            tok = (eng.sem, eng.cnt)
            if eng is self.pe:
                self.pending = False
        else:
            tok = (eng.sem, eng.cnt + 1)
            self.pending = True
        for b in reads:
            b.addr(tok)
        for b in writes:
            b.wtok = tok
            b.rtoks = {}
        return tok

    def dma(self, q, out, in_, reads=(), writes=(), dbuf=None, **kw):
        self.deps(q, reads, writes)
        if dbuf is None:
            dbuf = writes[0] if writes else reads[0]
        if dbuf.dsem is None:
            if self.free_dsems:
                dbuf.dsem = self.free_dsems.pop()
            else:
                self.nds += 1
                dbuf.dsem = [self.sem(f"dsem{self.nds}"), 0]
            self.live_dsems.append(dbuf.dsem)
        ds = dbuf.dsem
        if dbuf.dcnt:
            q.wait((ds[0], ds[1] * 16))
        ins = q.e.dma_start(out=out, in_=in_, **kw)
        ins.then_inc(ds[0], 16)
        ds[1] += 1
        dbuf.dcnt += 1
        tok = (ds[0], ds[1] * 16)
        for b in reads:
            b.addr(tok)
        for b in writes:
            b.wtok = tok
            b.rtoks = {}
        return tok

    def barrier(self):
        toks = []
        for e in self.engs:
            if e.cnt:
                toks.append((e.sem, e.cnt))
        for ds in self.live_dsems:
            if ds[1]:
                toks.append((ds[0], ds[1] * 16))
        for e in self.engs:
            for t in toks:
                e.wait(t)

    def release_dsems(self, keep=()):
        keep_ids = {id(b.dsem) for b in keep if b.dsem is not None}
        nl = []
        for ds in self.live_dsems:
            if id(ds) in keep_ids:
                nl.append(ds)
            else:
                self.free_dsems.append(ds)
        self.live_dsems = nl


import os
G = 4
QSCALE = 128.0 ** -0.5


def build(seqs, dbg=None):
    NT = sum(seqs)
    soff = [sum(seqs[:i]) for i in range(len(seqs))]
    nc = bass.Bass("TRN2", target_bir_lowering=False)
    din = lambda n, sh, dt=F32: nc.dram_tensor(n, sh, dt, kind="ExternalInput").ap()
    dsc = lambda n, sh, dt=F32: nc.dram_tensor(n, sh, dt, kind="Internal").ap()
    x_d = din("x", [NT, D])
    wg1 = din("ffn1_w_gate", [2, D, DFF]); wu1 = din("ffn1_w_up", [2, D, DFF]); wd1 = din("ffn1_w_down", [2, DFF, D])
    wg2 = din("ffn2_w_gate", [2, D, DFF]); wu2 = din("ffn2_w_up", [2, D, DFF]); wd2 = din("ffn2_w_down", [2, DFF, D])
    win_d = din("dn_w_in", [1, D, 12416]); wout_d = din("dn_w_out", [1, 4096, D]); wfn_d = din("fn_w_out", [1, D, D])
    gains_d = din("gainsT", [128, 6 * NKC])
    gfin_d = din("gfin_bc", [128, D])
    ident_d = din("ident", [128, 128], BF16)
    convw_d = din("convwT", [128, 64 * 5])
    alog_d = din("alog_bc", [128, 64]); dtb_d = din("dtb_bc", [128, 64]); onorm_d = din("onorm", [128, 1])
    cmask_d = din("cmask", [64, 6 * 64])
    cs256_d = din("cs256", [128, 2 * 512], BF16)
    tabs_d = [din(f"dft{S}", [S // 512, S // 512, 128, 4096], BF16) for S in sorted(set(seqs))]
    tab_of = {S: tabs_d[i] for i, S in enumerate(sorted(set(seqs)))}
    y_d = nc.dram_tensor("y", [NT, D], F32, kind="ExternalOutput").ap()

    FFN_SL = 66
    wscr_f = [dsc(f"wscr{f}", [FFN_SL, 128, 4096], BF16) for f in range(4)]
    wscr_in = dsc("wscr_in", [49, 128, 4096], BF16)
    wscr_out = dsc("wscr_out", [16, 128, 4096], BF16)
    wscr_fn = dsc("wscr_fn", [8, 128, 4096], BF16)
    xres = dsc("xres", [NT, D])
    projqkv = dsc("projqkv", [8192, NT])
    projz = dsc("projz", [4096, NT])
    gproc = dsc("gproc", [NT, 128])
    ogT = dsc("ogT", [4096, NT], BF16)
    ycs = dsc("ycs", [NT, 8 * 512], BF16)
    mixT = dsc("mixT", [D, NT], BF16)

    with ExitStack() as st:
        fw = FW(nc, st)
        op = fw.op
        pe, act, dve, pool, sp = fw.pe, fw.act, fw.dve, fw.pool, fw.sp
        uid = [0]

        def mksb(stack):
            def f(n, sh, dt=F32):
                uid[0] += 1
                return stack.enter_context(nc.sbuf_tensor(f"sb{uid[0]}_{n}", sh, dt))
            return f
        sb = mksb(st)

        const_bufs = []

        def cload(name, shape, src, dt=F32):
            t = sb(name, shape, dt)
            b = Buf(name)
            const_bufs.append(b)
            fw.dma(pool, t[:], src, writes=[b])
            return t, b
        gains, b_gains = cload("gains", [128, 6 * NKC], gains_d[:, :])
        gfin, b_gfin = cload("gfin", [128, D], gfin_d[:, :])
        ident, b_ident = cload("ident", [128, 128], ident_d[:, :], BF16)
        convw, b_convw = cload("convw", [128, 64 * 5], convw_d[:, :])
        nega, b_nega = cload("nega", [128, 64], alog_d[:, :])
        dtb, b_dtb = cload("dtb", [128, 64], dtb_d[:, :])
        onorm, b_onorm = cload("onorm", [128, 1], onorm_d[:, :])
        cmask, b_cmask = cload("cmask", [64, 6 * 64], cmask_d[:, :])
        cs256, b_cs256 = cload("cs256", [128, 2 * 512], cs256_d[:, :], BF16)
        ones32 = sb("ones32", [64, 128]); b_ones32 = Buf("ones32")
        onesbf = sb("onesbf", [128, 128], BF16); b_onesbf = Buf("onesbf")
        op(dve, lambda: nc.vector.memset(ones32[:], 1.0), writes=[b_ones32])
        op(dve, lambda: nc.vector.memset(onesbf[:], 1.0), writes=[b_onesbf])
        op(act, lambda: nc.scalar.activation(out=nega[:], in_=nega[:], func=AF.Exp), reads=[b_nega], writes=[b_nega])
        op(dve, lambda: nc.vector.tensor_scalar_mul(out=nega[:], in0=nega[:], scalar1=-1.0), reads=[b_nega], writes=[b_nega])
        cm = lambda i: cmask[:, i * 64:(i + 1) * 64]

        banks = [st.enter_context(nc.psum_tensor(f"pb{i}", [128, 512], F32)) for i in range(8)]
        b_bk = [Buf(f"bk{i}", excl=True) for i in range(8)]
        bk_n = [0]

        bkd_n = [0, 0]

        def nbank_d(d):
            i = 2 * d + bkd_n[d] % 2
            bkd_n[d] += 1
            return banks[i], b_bk[i]

        def nbank(lo=0, hi=8):
            i = lo + bk_n[0] % (hi - lo)
            bk_n[0] += 1
            return banks[i], b_bk[i]

        cvb = [Buf(f"cv{i}") for i in range(8)]
        cvn = [0]

        def conv(dst, src):
            b = cvb[cvn[0] % len(cvb)]
            cvn[0] += 1
            fw.dma(pool, dst, src, dbuf=b)

        def ffn_slab(f, kind, i):
            return wscr_f[f][kind * 22 + i]

        def conv_fm(dst_slab, w2d, c0, ncols=256):
            conv(dst_slab[:, 0:NKC * ncols].rearrange("p (kc c) -> p kc c", kc=NKC),
                 w2d[:, c0:c0 + ncols].rearrange("(kc p) c -> p kc c", p=128))

        def conv_tm(dst_slab, w2d, r0, n0):
            conv(dst_slab.rearrange("p (c n) -> p c n", c=4),
                 w2d[r0:r0 + 512, n0:n0 + 1024].rearrange("(c p) n -> p c n", p=128))

        ffn_w = [(wg1, wu1, wd1, 0), (wg2, wu2, wd2, 0), (wg1, wu1, wd1, 1), (wg2, wu2, wd2, 1)]
        run_ffn = dbg in (None, "ffn")
        run_dn = dbg in (None, "dnmix")
        run_fn = dbg in (None, "fnmix")
        def conv_ffn(f):
            wg, wu, wd, l = ffn_w[f]
            for s in range(22):
                conv_fm(ffn_slab(f, 0, s), wg[l], s * 256)
                yield
                conv_fm(ffn_slab(f, 1, s), wu[l], s * 256)
                yield
            for h in range(2):
                for s in range(11):
                    conv_tm(ffn_slab(f, 2, h * 11 + s), wd[l], s * 512, h * 1024)
                    yield

        def conv_late():
            for f in (1, 2, 3):
                yield from conv_ffn(f)
            for h in range(2):
                for s in range(8):
                    conv_tm(wscr_out[h * 8 + s], wout_d[0], s * 512, h * 1024)
                    yield
            for h in range(2):
                for s in range(4):
                    conv_tm(wscr_fn[h * 4 + s], wfn_d[0], s * 512, h * 1024)
                    yield
        late_gen = None
        if dbg is None:
            for _ in conv_ffn(0):
                pass
            for s in range(48):
                conv_fm(wscr_in[s], win_d[0], s * 256)
            conv_fm(wscr_in[48], win_d[0], 12288, 128)
            late_gen = conv_late()
        else:
            if run_ffn:
                for f in range(4):
                    for _ in conv_ffn(f):
                        pass
            if run_dn:
                for s in range(48):
                    conv_fm(wscr_in[s], win_d[0], s * 256)
                conv_fm(wscr_in[48], win_d[0], 12288, 128)
                for h in range(2):
                    for s in range(8):
                        conv_tm(wscr_out[h * 8 + s], wout_d[0], s * 512, h * 1024)
            if run_fn:
                for h in range(2):
                    for s in range(4):
                        conv_tm(wscr_fn[h * 4 + s], wfn_d[0], s * 512, h * 1024)
        fw.barrier()
        fw.release_dsems(keep=cvb)

        class Row:
            pass

        def row_ctx(ph):
            R = Row()
            sbp = mksb(ph)
            xt = sbp("xt", [128, NSUB, D]); b_x = Buf("x")
            hn = [sbp(f"hn{i}", [128, D], BF16) for i in range(2)]; b_hn = [Buf(f"hn{i}") for i in range(2)]
            junk = sbp("junk", [128, D], BF16); b_junk = Buf("junk")
            stat = sbp("stat", [128, 16]); b_stat = Buf("stat")
            hT = sbp("hT", [128, NKC, T], BF16); b_hT = Buf("hT")
            aT = sbp("aT", [128, NFC, T], BF16); b_aT = Buf("aT")
            sg = [sbp(f"sg{i}", [128, T]) for i in range(2)]; b_sg = [Buf(f"sg{i}") for i in range(2)]
            stg = [sbp(f"stg{i}", [128, 2, T]) for i in range(2)]; b_stg = [Buf(f"stg{i}") for i in range(2)]
            NSL = 5
            wring = [sbp(f"wr{i}", [128, 4096], BF16) for i in range(NSL)]; b_wr = [Buf(f"wr{i}") for i in range(NSL)]
            wr_n = [0]
            R.xt, R.b_x, R.hT, R.b_hT, R.aT, R.b_aT = xt, b_x, hT, b_hT, aT, b_aT

            def wload(slab_ap):
                i = wr_n[0] % NSL
                wr_n[0] += 1
                fw.dma(sp, wring[i][:], slab_ap, writes=[b_wr[i]])
                return wring[i], b_wr[i]
            R.wload = wload

            def rstd_of(j):
                op(dve, lambda: nc.vector.memset(stat[:, j:j + 1], 0.0), writes=[b_stat])
                op(act, lambda: nc.scalar.activation(out=junk[:], in_=xt[:, j, :], func=AF.Square, accum_out=stat[:, j:j + 1]),
                   reads=[b_x], writes=[b_junk, b_stat])
                op(dve, lambda: nc.vector.tensor_scalar(out=stat[:, 4 + j:5 + j], in0=stat[:, j:j + 1], scalar1=1.0 / D, scalar2=EPS,
                                                         op0=ALU.mult, op1=ALU.add), reads=[b_stat], writes=[b_stat])
                op(act, lambda: nc.scalar.activation(out=stat[:, 8 + j:9 + j], in_=stat[:, 4 + j:5 + j], func=AF.Sqrt),
                   reads=[b_stat], writes=[b_stat])
                op(dve, lambda: nc.vector.reciprocal(out=stat[:, 12 + j:13 + j], in_=stat[:, 8 + j:9 + j]), reads=[b_stat], writes=[b_stat])

            def norm_transpose(gi):
                for j in range(NSUB):
                    h, bh = hn[j % 2], b_hn[j % 2]
                    rstd_of(j)
                    op(act, lambda: nc.scalar.activation(out=h[:], in_=xt[:, j, :], func=AF.Copy, scale=stat[:, 12 + j:13 + j]),
                       reads=[b_x, b_stat], writes=[bh])
                    for q in range(4):
                        bank, bb = nbank()
                        for c in range(4):
                            kc = q * 4 + c
                            op(pe, lambda: nc.tensor.matmul(bank[:, c * 128:(c + 1) * 128], lhsT=h[:, kc * 128:(kc + 1) * 128], rhs=ident[:],
                                                            start=True, stop=True), reads=[bh, b_ident], writes=[bb], inc=(c == 3))
                        op(dve, lambda: nc.vector.tensor_tensor(
                            out=hT[:, q * 4:(q + 1) * 4, j * 128:(j + 1) * 128],
                            in0=bank[:].rearrange("p (c t) -> p c t", c=4),
                            in1=gains[:, gi * NKC + q * 4: gi * NKC + q * 4 + 4].unsqueeze(2).to_broadcast([128, 4, 128]),
                            op=ALU.mult), reads=[bb, b_gains], writes=[b_hT])
            R.norm_transpose = norm_transpose

            def tm_matmul(lhs_fn, lhs_bufs, nk, slab_fn, scale):
                for half in range(2):
                    accs = [[(banks[j * 2 + c], b_bk[j * 2 + c]) for c in range(2)] for j in range(NSUB)]
                    for s in range(nk // 4):
                        slot, bs = wload(slab_fn(half, s))
                        for c4 in range(4):
                            kc = 4 * s + c4
                            for j in range(NSUB):
                                for c in range(2):
                                    bank, bb = accs[j][c]
                                    op(pe, lambda: nc.tensor.matmul(bank[:], lhsT=lhs_fn(kc, j),
                                                                    rhs=slot[:, c4 * 1024 + c * 512: c4 * 1024 + (c + 1) * 512],
                                                                    start=(kc == 0), stop=(kc == nk - 1)),
                                       reads=[bs] + lhs_bufs, writes=[bb],
                                       inc=(kc == nk - 1) or (c4 == 3 and j == NSUB - 1 and c == 1))
                    for j in range(NSUB):
                        for c in range(2):
                            bank, bb = accs[j][c]
                            col = half * 1024 + c * 512
                            op(dve, lambda: nc.vector.scalar_tensor_tensor(out=xt[:, j, col:col + 512], in0=bank[:], scalar=float(scale),
                                                                           in1=xt[:, j, col:col + 512], op0=ALU.mult, op1=ALU.add),
                               reads=[bb, b_x], writes=[b_x])
            R.tm_matmul = tm_matmul

            def ffn(f, gi):
                norm_transpose(gi)
                for s in range(22):
                    slg, bsg_ = wload(ffn_slab(f, 0, s))
                    slu, bsu_ = wload(ffn_slab(f, 1, s))
                    for c2 in range(2):
                        fc = 2 * s + c2
                        bg, bbg = nbank()
                        bu, bbu = nbank()
                        for (bank, bb, sl, bsl) in ((bg, bbg, slg, bsg_), (bu, bbu, slu, bsu_)):
                            for kc in range(NKC):
                                op(pe, lambda: nc.tensor.matmul(bank[:], lhsT=sl[:, kc * 256 + c2 * 128: kc * 256 + (c2 + 1) * 128],
                                                                rhs=hT[:, kc, :], start=(kc == 0), stop=(kc == NKC - 1)),
                                   reads=[bsl, b_hT], writes=[bb], inc=(kc == NKC - 1))
                        sgt, bsgt = sg[fc % 2], b_sg[fc % 2]
                        op(act, lambda: nc.scalar.activation(out=sgt[:], in_=bg[:], func=AF.Silu), reads=[bbg], writes=[bsgt])
                        op(dve, lambda: nc.vector.tensor_tensor(out=aT[:, fc, :], in0=sgt[:], in1=bu[:], op=ALU.mult),
                           reads=[bsgt, bbu], writes=[b_aT])
                tm_matmul(lambda kc, j: aT[:, kc, j * 128:(j + 1) * 128], [b_aT], NFC,
                          lambda half, s: ffn_slab(f, 2, half * 11 + s), 0.5)
            R.ffn = ffn

            def load_x(src, t0):
                fw.dma(pool, xt[:], src[t0:t0 + T, :].rearrange("(j p) d -> p j d", p=128), writes=[b_x])

            def store_x(dst, t0):
                fw.dma(pool, dst[t0:t0 + T, :].rearrange("(j p) d -> p j d", p=128), xt[:], reads=[b_x])
            R.load_x, R.store_x = load_x, store_x

            def final_norm():
                for j in range(NSUB):
                    rstd_of(j)
                    op(dve, lambda: nc.vector.scalar_tensor_tensor(out=xt[:, j, :], in0=xt[:, j, :], scalar=stat[:, 12 + j:13 + j],
                                                                   in1=gfin[:], op0=ALU.mult, op1=ALU.mult),
                       reads=[b_x, b_stat, b_gfin], writes=[b_x])
            R.final_norm = final_norm

            gst = sbp("gst", [128, NSUB, 128]); b_gst = Buf("gst")
            gtmp = sbp("gtmp", [128, 64]); b_gtmp = Buf("gtmp")

            def dn_inproj(t0):
                for s in range(48):
                    slot, bs = wload(wscr_in[s])
                    sgb, bsgb = stg[s % 2], b_stg[s % 2]
                    for c2 in range(2):
                        bank, bb = nbank()
                        for kc in range(NKC):
                            op(pe, lambda: nc.tensor.matmul(bank[:], lhsT=slot[:, kc * 256 + c2 * 128: kc * 256 + (c2 + 1) * 128],
                                                            rhs=hT[:, kc, :], start=(kc == 0), stop=(kc == NKC - 1)),
                               reads=[bs, b_hT], writes=[bb], inc=(kc == NKC - 1))
                        if c2 == 0:
                            op(act, lambda: nc.scalar.copy(out=sgb[:, c2, :], in_=bank[:]), reads=[bb], writes=[bsgb])
                        else:
                            op(dve, lambda: nc.vector.tensor_copy(out=sgb[:, c2, :], in_=bank[:]), reads=[bb], writes=[bsgb])
                    if s < 32:
                        dst = projqkv[s * 256:(s + 1) * 256, t0:t0 + T]
                    else:
                        dst = projz[(s - 32) * 256:(s - 31) * 256, t0:t0 + T]
                    fw.dma(pool, dst.rearrange("(c p) t -> p c t", p=128), sgb[:], reads=[bsgb])
                    if late_gen is not None and s % 2 == 1:
                        next(late_gen, None)
                slot, bs = wload(wscr_in[48])
                for j in range(NSUB):
                    bank, bb = nbank()
                    for kc in range(NKC):
                        op(pe, lambda: nc.tensor.matmul(bank[:, 0:128], lhsT=hT[:, kc, j * 128:(j + 1) * 128],
                                                        rhs=slot[:, kc * 128:(kc + 1) * 128], start=(kc == 0), stop=(kc == NKC - 1)),
                           reads=[bs, b_hT], writes=[bb], inc=(kc == NKC - 1))
                    ps4 = bank[:, 0:128].rearrange("p (d k h) -> p d k h", d=2, k=2)
                    g4 = gst[:, j, :].rearrange("p (d k h) -> p d k h", d=2, k=2)
                    gt3 = gtmp[:].rearrange("p (d h) -> p d h", d=2)
                    op(dve, lambda: nc.vector.tensor_tensor(out=gt3, in0=ps4[:, :, 0, :], in1=dtb[:].rearrange("p (d h) -> p d h", d=2),
                                                            op=ALU.add), reads=[bb, b_dtb], writes=[b_gtmp])
                    op(act, lambda: nc.scalar.activation(out=gtmp[:], in_=gtmp[:], func=AF.Exp), reads=[b_gtmp], writes=[b_gtmp])
                    op(dve, lambda: nc.vector.tensor_scalar_add(out=gtmp[:], in0=gtmp[:], scalar1=1.0), reads=[b_gtmp], writes=[b_gtmp])
                    op(act, lambda: nc.scalar.activation(out=gtmp[:], in_=gtmp[:], func=AF.Ln), reads=[b_gtmp], writes=[b_gtmp])
                    op(dve, lambda: nc.vector.tensor_tensor(out=g4[:, :, 0, :], in0=gt3, in1=nega[:].rearrange("p (d h) -> p d h", d=2),
                                                            op=ALU.mult), reads=[b_gtmp, b_nega], writes=[b_gst])
                    op(act, lambda: nc.scalar.activation(out=g4[:, :, 1, :], in_=ps4[:, :, 1, :], func=AF.Sigmoid), reads=[bb], writes=[b_gst])
                fw.dma(pool, gproc[t0:t0 + T, :].rearrange("(j p) c -> p j c", p=128), gst[:], reads=[b_gst])
            R.dn_inproj = dn_inproj

            def dn_outproj(t0):
                fw.dma(pool, aT[:, 0:32, :], ogT[:, t0:t0 + T].rearrange("(kc p) t -> p kc t", p=128), writes=[b_aT])
                tm_matmul(lambda kc, j: aT[:, kc, j * 128:(j + 1) * 128], [b_aT], 32, lambda half, s: wscr_out[half * 8 + s], 1.0)
            R.dn_outproj = dn_outproj

            yst = [sbp(f"yst{i}", [128, 2, 512], BF16) for i in range(2)]; b_yst = [Buf(f"yst{i}") for i in range(2)]

            def fn_step1(t0):
                n = 0
                for j in range(NSUB):
                    for g2 in range(4):
                        ys, bys = yst[n % 2], b_yst[n % 2]
                        n += 1
                        for gg in range(2):
                            g = g2 * 2 + gg
                            bank, bb = nbank()
                            for k2 in range(2):
                                op(pe, lambda: nc.tensor.matmul(bank[:], lhsT=hT[:, 2 * g + k2, j * 128:(j + 1) * 128],
                                                                rhs=cs256[:, k2 * 512:(k2 + 1) * 512], start=(k2 == 0), stop=(k2 == 1)),
                                   reads=[b_hT, b_cs256], writes=[bb], inc=(k2 == 1))
                            if gg == 0:
                                op(act, lambda: nc.scalar.copy(out=ys[:, gg, :], in_=bank[:]), reads=[bb], writes=[bys])
                            else:
                                op(dve, lambda: nc.vector.tensor_copy(out=ys[:, gg, :], in_=bank[:]), reads=[bb], writes=[bys])
                        fw.dma(pool, ycs[t0 + j * 128:t0 + (j + 1) * 128, g2 * 1024:(g2 + 1) * 1024].rearrange("p (a b) -> p a b", a=2),
                               ys[:], reads=[bys])
            R.fn_step1 = fn_step1

            def fn_outproj(t0):
                fw.dma(pool, hT[:], mixT[:, t0:t0 + T].rearrange("(kc p) t -> p kc t", p=128), writes=[b_hT])
                tm_matmul(lambda kc, j: hT[:, kc, j * 128:(j + 1) * 128], [b_hT], NKC, lambda half, s: wscr_fn[half * 4 + s], 1.0)
            R.fn_outproj = fn_outproj
            return R

        def dn_core(S, s0):
            NCH = S // CH
            NG = NCH // G
            CB = 512
            OFFS = 0
            with ExitStack() as ph:
                sbp = mksb(ph)
                qT = sbp("qT", [128, S], BF16); b_qT = Buf("qT")
                kT = sbp("kT", [128, S], BF16); b_kT = Buf("kT")
                Ktok = sbp("Ktok", [64, NCH, 128], BF16); b_Ktok = Buf("Ktok")
                Vtok = sbp("Vtok", [64, NCH, 2, 128], BF16); b_Vtok = Buf("Vtok")
                oT = sbp("oT", [128, 2 * S]); b_oT = Buf("oT")
                gall = oT[0:64, :].rearrange("p (n c) -> p n c", c=128)
                g8 = sbp("g8", [64, 4, NCH, 2]); b_g8 = Buf("g8")
                class PS:
                    pass
                psets = []
                for i in range(2):
                    p_ = PS()
                    p_.raw = sbp(f"raw{i}", [128, CB + 4]); p_.b_raw = Buf(f"raw{i}")
                    p_.acc = sbp(f"acc{i}", [128, CB]); p_.b_acc = Buf(f"acc{i}")
                    p_.sil = sbp(f"sil{i}", [128, CB]); p_.b_sil = Buf(f"sil{i}")
                    p_.silb = sbp(f"silb{i}", [128, CB], BF16); p_.b_silb = Buf(f"silb{i}")
                    p_.rn = sbp(f"rn{i}", [128, 512]); p_.b_rn = Buf(f"rn{i}")
                    p_.lo, p_.hi = 4 * i, 4 * i + 4
                    psets.append(p_)
                Sst = sbp("Sst", [128, 4, 128]); b_Sst = Buf("Sst")
                Sbf = sbp("Sbf", [128, 4, 128], BF16); b_Sbf = Buf("Sbf")
                vn = sbp("vn", [64, 4, 128], BF16); b_vnall = Buf("vnall")
                ogs = [sbp(f"ogs{i}", [128, 512], BF16) for i in range(2)]; b_ogs = [Buf(f"ogs{i}") for i in range(2)]
                class BA:
                    pass
                bas = [[None, None], [None, None]]
                for d, par_ in ((0, 0), (0, 1), (1, 0), (1, 1)):
                    a = BA()
                    dn_ = f"{d}{par_}"
                    a.TT = sbp(f"TT{dn_}", [64, 8, 64], BF16); a.b_TT = Buf(f"TT{dn_}")
                    a.vb = sbp(f"vb{dn_}", [64, G, 2, 128], BF16); a.b_vb = Buf(f"vb{dn_}")
                    a.kgl = sbp(f"kgl{dn_}", [64, G, 2, 128], BF16); a.b_kgl = Buf(f"kgl{dn_}")
                    a.nwT = sbp(f"nwT{dn_}", [128, 8, 64], BF16); a.b_nwT = Buf(f"nwT{dn_}")
                    a.qgT = sbp(f"qgT{dn_}", [128, 8, 64], BF16); a.b_qgT = Buf(f"qgT{dn_}")
                    a.qkT = sbp(f"qkT{dn_}", [64, 8, 64], BF16); a.b_qkT = Buf(f"qkT{dn_}")
                    a.egl = sbp(f"egl{dn_}", [128, 8]); a.b_egl = Buf(f"egl{dn_}")
                    bas[d][par_] = a
                eg4 = [sbp(f"eg4{i}", [128, G, 4]) for i in range(2)]; b_eg4 = [Buf(f"eg4{i}") for i in range(2)]
                class TMP:
                    pass
                tmps = []
                for d in range(2):
                    t = TMP()
                    t.Mt = sbp(f"Mt{d}", [64, 512]); t.b_Mt = Buf(f"Mt{d}")
                    t.Dt, t.b_Dt, t.Ld, t.b_Ld = t.Mt, t.b_Mt, t.Mt, t.b_Mt
                    t.dec = sbp(f"dec{d}", [64, 512]); t.b_dec = Buf(f"dec{d}")
                    t.As = sbp(f"As{d}", [64, G, 64]); t.b_As = Buf(f"As{d}")
                    t.Pb = [sbp(f"Pb{d}{i}", [64, 8, 64], BF16) for i in range(2)]; t.b_Pb = [Buf(f"Pb{d}{i}") for i in range(2)]
                    t.Nb = [sbp(f"Nb{d}{i}", [64, 8, 64], BF16) for i in range(2)]; t.b_Nb = [Buf(f"Nb{d}{i}") for i in range(2)]
                    t.Rb = [sbp(f"Rb{d}{i}", [64, 8, 64], BF16) for i in range(2)]; t.b_Rb = [Buf(f"Rb{d}{i}") for i in range(2)]
                    t.qk = sbp(f"qk{d}", [64, 8, 64], BF16); t.b_qk = Buf(f"qk{d}")
                    t.kbg = sbp(f"kbg{d}", [64, G, 2, 128], BF16); t.b_kbg = Buf(f"kbg{d}")
                    t.egr = sbp(f"egr{d}", [128, 512]); t.b_egr = Buf(f"egr{d}")
                    t.sm = sbp(f"sm{d}", [64, 6, 8]); t.b_sm = Buf(f"sm{d}")
                    t.qks = sbp(f"qks{d}", [64, G, 64]); t.b_qks = Buf(f"qks{d}")
                    t.Asb = sbp(f"Asb{d}", [64, G, 2, 64]); t.b_Asb = Buf(f"Asb{d}")
                    tmps.append(t)
                v4 = lambda t: t[:].rearrange("p (n v c) -> p n v c", n=G, v=2) if len(t.shape) == 2 else t[:].rearrange("p (n v) c -> p n v c", v=2)

                def conv_silu(P, rows_ap, cc, cb, out_t, b_out):
                    t0 = cb * CB
                    r, br, acc, b_acc = P.raw, P.b_raw, P.acc, P.b_acc
                    lo, hi = max(t0 - 2, 0), min(t0 + CB + 2, S)
                    if t0 == 0:
                        op(dve, lambda: nc.vector.memset(r[:, 0:2], 0.0), writes=[br])
                    if t0 + CB == S:
                        op(dve, lambda: nc.vector.memset(r[:, CB + 2:CB + 4], 0.0), writes=[br])
                    fw.dma(pool, r[:, lo - (t0 - 2):hi - (t0 - 2)], rows_ap[:, s0 + lo:s0 + hi], writes=[br])
                    yield
                    w = lambda tap: convw[:, cc * 5 + tap:cc * 5 + tap + 1]
                    op(act, lambda: nc.scalar.activation(out=acc[:], in_=r[:, 0:CB], func=AF.Copy, scale=w(0)),
                       reads=[br, b_convw], writes=[b_acc])
                    yield
                    for tap in range(1, 5):
                        op(dve, lambda: nc.vector.scalar_tensor_tensor(out=acc[:], in0=r[:, tap:tap + CB], scalar=w(tap), in1=acc[:],
                                                                       op0=ALU.mult, op1=ALU.add), reads=[br, b_convw, b_acc], writes=[b_acc])
                        yield
                    op(act, lambda: nc.scalar.activation(out=out_t[:], in_=acc[:], func=AF.Silu), reads=[b_acc], writes=[b_out])
                    yield

                def l2norm_to(P, dst, b_dst, cb, scale):
                    sil, b_sil, silb, b_silb, rn, b_rn = P.sil, P.b_sil, P.silb, P.b_silb, P.rn, P.b_rn
                    op(dve, lambda: nc.vector.tensor_tensor(out=silb[:], in0=sil[:], in1=sil[:], op=ALU.mult), reads=[b_sil], writes=[b_silb])
                    yield
                    for h in range(CB // 512):
                        bank, bb = nbank(P.lo, P.hi)
                        op(pe, lambda: nc.tensor.matmul(bank[:], lhsT=onesbf[:], rhs=silb[:, h * 512:(h + 1) * 512], start=True, stop=True),
                           reads=[b_onesbf, b_silb], writes=[bb])
                        yield
                        op(dve, lambda: nc.vector.tensor_scalar_add(out=rn[:], in0=bank[:], scalar1=EPS), reads=[bb], writes=[b_rn])
                        yield
                        op(act, lambda: nc.scalar.activation(out=rn[:], in_=rn[:], func=AF.Ln), reads=[b_rn], writes=[b_rn])
                        yield
                        op(act, lambda: nc.scalar.activation(out=rn[:], in_=rn[:], func=AF.Exp, scale=-0.5), reads=[b_rn], writes=[b_rn])
                        yield
                        c0 = cb * CB + h * 512
                        op(dve, lambda: nc.vector.scalar_tensor_tensor(out=dst[:, c0:c0 + 512], in0=sil[:, h * 512:(h + 1) * 512], scalar=float(scale),
                                                                       in1=rn[:], op0=ALU.mult, op1=ALU.mult), reads=[b_sil, b_rn], writes=[b_dst])
                        yield

                def to_tok(P, srcT, b_src, c_off, dst_fn, b_dst, n0, nn):
                    for q in range(0, nn, 4):
                        bank, bb = nbank(P.lo, P.hi)
                        for c in range(4):
                            col = c_off + (q + c) * 64
                            op(pe, lambda: nc.tensor.matmul(bank[0:64, c * 128:(c + 1) * 128], lhsT=srcT[:, col:col + 64], rhs=ident[:],
                                                            start=True, stop=True), reads=[b_src, b_ident], writes=[bb], inc=(c == 3))
                        yield
                        op(act, lambda: nc.scalar.copy(out=dst_fn(n0 + q), in_=bank[0:64, :].rearrange("p (c k) -> p c k", c=4)),
                           reads=[bb], writes=[b_dst])
                        yield

                def rr(gens):
                    gens = list(gens)
                    while gens:
                        for g_ in list(gens):
                            try:
                                next(g_)
                            except StopIteration:
                                gens.remove(g_)

                def chain_q(P, hk):
                    for cb in range(S // CB):
                        yield from conv_silu(P, projqkv[hk * 128:(hk + 1) * 128, :], hk, cb, P.sil, P.b_sil)
                        yield from l2norm_to(P, qT, b_qT, cb, QSCALE)

                def chain_k(P, hk):
                    nn = CB // CH
                    for cb in range(S // CB):
                        yield from conv_silu(P, projqkv[2048 + hk * 128:2048 + (hk + 1) * 128, :], 16 + hk, cb, P.sil, P.b_sil)
                        yield from l2norm_to(P, kT, b_kT, cb, 1.0)
                        yield from to_tok(P, kT, b_kT, cb * CB, lambda n: Ktok[:, n:n + 4, :], b_Ktok, cb * nn, nn)

                def chain_v(P, hk, v):
                    nn = CB // CH
                    vh = 2 * hk + v
                    for cb in range(S // CB):
                        yield from conv_silu(P, projqkv[4096 + vh * 128:4096 + (vh + 1) * 128, :], 32 + vh, cb, P.silb, P.b_silb)
                        yield from to_tok(P, P.silb, P.b_silb, 0, lambda n: Vtok[:, n:n + 4, v, :], b_Vtok, cb * nn, nn)

                def chain_fin(P, hk, v):
                    vh = 2 * hk + v
                    r, br, silb, b_silb, rn, b_rn = P.raw, P.b_raw, P.silb, P.b_silb, P.rn, P.b_rn
                    for h in range(S // 512):
                        osl = oT[:, v * S + h * 512:v * S + (h + 1) * 512]
                        fw.dma(pool, r[:, 0:512], projz[vh * 128:(vh + 1) * 128, s0 + h * 512:s0 + (h + 1) * 512], writes=[br])
                        yield
                        op(dve, lambda: nc.vector.tensor_tensor(out=silb[:, 0:512], in0=osl, in1=osl, op=ALU.mult), reads=[b_oT], writes=[b_silb])
                        yield
                        bank, bb = nbank(P.lo, P.hi)
                        op(pe, lambda: nc.tensor.matmul(bank[:], lhsT=onesbf[:], rhs=silb[:, 0:512], start=True, stop=True),
                           reads=[b_onesbf, b_silb], writes=[bb])
                        yield
                        op(dve, lambda: nc.vector.tensor_scalar(out=rn[:], in0=bank[:], scalar1=1.0 / 128, scalar2=EPS, op0=ALU.mult, op1=ALU.add),
                           reads=[bb], writes=[b_rn])
                        yield
                        op(act, lambda: nc.scalar.activation(out=rn[:], in_=rn[:], func=AF.Ln), reads=[b_rn], writes=[b_rn])
                        yield
                        op(act, lambda: nc.scalar.activation(out=rn[:], in_=rn[:], func=AF.Exp, scale=-0.5), reads=[b_rn], writes=[b_rn])
                        yield
                        op(act, lambda: nc.scalar.activation(out=r[:, 0:512], in_=r[:, 0:512], func=AF.Silu), reads=[br], writes=[br])
                        yield
                        op(dve, lambda: nc.vector.tensor_tensor(out=rn[:], in0=rn[:], in1=osl, op=ALU.mult), reads=[b_rn, b_oT], writes=[b_rn])
                        yield
                        og, bog = ogs[v], b_ogs[v]
                        op(dve, lambda: nc.vector.scalar_tensor_tensor(out=og[:], in0=rn[:], scalar=onorm[:, 0:1], in1=r[:, 0:512],
                                                                       op0=ALU.mult, op1=ALU.mult), reads=[b_rn, b_onorm, br], writes=[bog])
                        yield
                        fw.dma(pool, ogT[vh * 128:(vh + 1) * 128, s0 + h * 512:s0 + (h + 1) * 512], og[:], reads=[bog])
                        yield

                def precompute(hk, d, cg, par):
                    a = bas[d][par]
                    tm_ = tmps[d]
                    Mt, b_Mt, Dt, b_Dt, Ld, b_Ld, dec, b_dec, As, b_As = tm_.Mt, tm_.b_Mt, tm_.Dt, tm_.b_Dt, tm_.Ld, tm_.b_Ld, tm_.dec, tm_.b_dec, tm_.As, tm_.b_As
                    Pb, b_Pb, Nb, b_Nb, Rb, b_Rb, qk, b_qk = tm_.Pb, tm_.b_Pb, tm_.Nb, tm_.b_Nb, tm_.Rb, tm_.b_Rb, tm_.qk, tm_.b_qk
                    kbg, b_kbg, egr, b_egr, sm, b_sm = tm_.kbg, tm_.b_kbg, tm_.egr, tm_.b_egr, tm_.sm, tm_.b_sm
                    n0 = cg * G
                    c0 = n0 * CH
                    gv = g8[:, d, n0:n0 + G, :]
                    bv = g8[:, 2 + d, n0:n0 + G, :]
                    gv2 = gv.rearrange("p n v -> p (n v)")
                    smv = lambda i: sm[:, i, :].rearrange("p (n v) -> p n v", v=2)
                    bA, bbA = nbank_d(d)
                    bQ, bbQ = nbank_d(d)
                    for n in range(G):
                        cs = slice(c0 + n * CH, c0 + (n + 1) * CH)
                        op(pe, lambda: nc.tensor.matmul(bA[0:64, n * 64:(n + 1) * 64], lhsT=kT[:, cs], rhs=kT[:, cs], start=True, stop=True),
                           reads=[b_kT], writes=[bbA], inc=(n == G - 1))
                        if not fw.pending:
                            yield
                    for n in range(G):
                        cs = slice(c0 + n * CH, c0 + (n + 1) * CH)
                        op(pe, lambda: nc.tensor.matmul(bQ[0:64, n * 64:(n + 1) * 64], lhsT=qT[:, cs], rhs=kT[:, cs], start=True, stop=True),
                           reads=[b_qT, b_kT], writes=[bbQ], inc=(n == G - 1))
                        if not fw.pending:
                            yield
                    op(dve, lambda: nc.vector.tensor_tensor(out=As[:], in0=bA[0:64, 0:G * 64].rearrange("p (n c) -> p n c", n=G),
                                                            in1=cm(4 + d).unsqueeze(1).to_broadcast([64, G, 64]), op=ALU.mult),
                       reads=[bbA, b_cmask], writes=[b_As])
                    if not fw.pending:
                        yield
                    op(dve, lambda: nc.vector.tensor_tensor(out=tm_.Asb[:], in0=As[:].unsqueeze(2).to_broadcast([64, G, 2, 64]),
                                                              in1=bv.unsqueeze(3).to_broadcast([64, G, 2, 64]), op=ALU.mult),
                       reads=[b_As, b_g8], writes=[tm_.b_Asb])
                    if not fw.pending:
                        yield
                    op(act, lambda: nc.scalar.copy(out=tm_.qks[:].rearrange("p n c -> p (n c)"), in_=bQ[0:64, 0:G * 64]), reads=[bbQ], writes=[tm_.b_qks])
                    if not fw.pending:
                        yield
                    bG, bbG = nbank_d(d)
                    op(pe, lambda: nc.tensor.matmul(bG[0:64, 0:8], lhsT=cm(d), rhs=gv2, start=True, stop=True),
                       reads=[b_cmask, b_g8], writes=[bbG], inc=False)
                    if not fw.pending:
                        yield
                    op(pe, lambda: nc.tensor.matmul(bG[:, 8:16], lhsT=ones32[:], rhs=gv2, start=True, stop=True),
                       reads=[b_ones32, b_g8], writes=[bbG])
                    if not fw.pending:
                        yield
                    op(dve, lambda: nc.vector.tensor_tensor(out=v4(Mt), in0=cm(d).unsqueeze(1).unsqueeze(1).to_broadcast([64, G, 2, 64]),
                                                            in1=gv.unsqueeze(3).to_broadcast([64, G, 2, 64]), op=ALU.mult),
                       reads=[b_cmask, b_g8], writes=[b_Mt])
                    if not fw.pending:
                        yield
                    bR, bbR = nbank_d(d)
                    op(pe, lambda: nc.tensor.matmul(bR[:], lhsT=ones32[:], rhs=Mt[:], start=True, stop=True),
                       reads=[b_ones32, b_Mt], writes=[bbR])
                    if not fw.pending:
                        yield
                    op(act, lambda: nc.scalar.copy(out=sm[:, 0, :], in_=bG[0:64, 0:8]), reads=[bbG], writes=[b_sm])
                    if not fw.pending:
                        yield
                    if d == 0:
                        op(act, lambda: nc.scalar.activation(out=eg4[par][:, :, 0:2], in_=bG[:, 8:16].rearrange("p (n v) -> p n v", v=2), func=AF.Exp),
                           reads=[bbG], writes=[b_eg4[par]])
                    else:
                        for nl_ in range(G):
                            op(act, lambda: nc.scalar.activation(out=eg4[par][:, G - 1 - nl_, 2:4], in_=bG[:, 8 + 2 * nl_:10 + 2 * nl_], func=AF.Exp),
                               reads=[bbG], writes=[b_eg4[par]])
                    if not fw.pending:
                        yield
                    op(act, lambda: nc.scalar.copy(out=sm[:, 5, :], in_=bG[0:64, 8:16]), reads=[bbG], writes=[b_sm])
                    if not fw.pending:
                        yield
                    op(dve, lambda: nc.vector.tensor_tensor(out=sm[:, 4, :], in0=sm[:, 5, :], in1=sm[:, 0, :], op=ALU.subtract),
                       reads=[b_sm], writes=[b_sm])
                    if not fw.pending:
                        yield
                    op(act, lambda: nc.scalar.activation(out=sm[:, 3, :], in_=sm[:, 4, :], func=AF.Exp), reads=[b_sm], writes=[b_sm])
                    if not fw.pending:
                        yield
                    op(act, lambda: nc.scalar.activation(out=sm[:, 1, :], in_=sm[:, 0, :], func=AF.Exp), reads=[b_sm], writes=[b_sm])
                    if not fw.pending:
                        yield
                    op(dve, lambda: nc.vector.tensor_tensor(out=smv(2), in0=smv(1), in1=bv, op=ALU.mult), reads=[b_sm, b_g8], writes=[b_sm])
                    if not fw.pending:
                        yield
                    op(dve, lambda: nc.vector.tensor_tensor(out=v4(Dt), in0=smv(0).unsqueeze(3).to_broadcast([64, G, 2, 64]),
                                                            in1=bR[0:64, :].rearrange("p (n v c) -> p n v c", n=G, v=2), op=ALU.subtract),
                       reads=[b_sm, bbR], writes=[b_Dt])
                    if not fw.pending:
                        yield
                    op(dve, lambda: nc.vector.tensor_tensor(out=v4(Dt), in0=v4(Dt), in1=cm(2 + d).unsqueeze(1).unsqueeze(1).to_broadcast([64, G, 2, 64]),
                                                            op=ALU.add), reads=[b_Dt, b_cmask], writes=[b_Dt])
                    if not fw.pending:
                        yield
                    op(act, lambda: nc.scalar.activation(out=dec[:], in_=Dt[:], func=AF.Exp), reads=[b_Dt], writes=[b_dec])
                    if not fw.pending:
                        yield
                    op(act, lambda: nc.scalar.activation(out=egr[:], in_=bR[:], func=AF.Exp), reads=[bbR], writes=[b_egr])
                    if not fw.pending:
                        yield
                    op(dve, lambda: nc.vector.tensor_tensor(out=v4(Pb[0]), in0=v4(dec), in1=tm_.Asb[:], op=ALU.mult),
                       reads=[b_dec, tm_.b_Asb], writes=[b_Pb[0]])
                    if not fw.pending:
                        yield
                    op(dve, lambda: nc.vector.tensor_tensor(out=v4(qk), in0=v4(dec),
                                                            in1=tm_.qks[:].unsqueeze(2).to_broadcast([64, G, 2, 64]),
                                                            op=ALU.mult), reads=[b_dec, tm_.b_qks], writes=[b_qk])
                    if not fw.pending:
                        yield
                    for (src, bsrc, dst, bdst) in ((Pb[0], b_Pb[0], Nb[0], b_Nb[0]), (qk, b_qk, a.qkT, a.b_qkT)):
                        bank, bb = nbank_d(d)
                        for c in range(8):
                            op(pe, lambda: nc.tensor.matmul(bank[0:64, c * 64:(c + 1) * 64], lhsT=src[:, c, :], rhs=ident[0:64, 0:64],
                                                            start=True, stop=True), reads=[bsrc, b_ident], writes=[bb], inc=(c == 7))
                            if not fw.pending:
                                yield
                        op(act, lambda: nc.scalar.copy(out=dst[:].rearrange("p a b -> p (a b)"), in_=bank[0:64, :]), reads=[bb], writes=[bdst])
                        if not fw.pending:
                            yield
                        if src is Pb[0]:
                            op(dve, lambda: nc.vector.tensor_tensor(out=Rb[0][:], in0=ident[0:64, 0:64].unsqueeze(1).to_broadcast([64, 8, 64]),
                                                                    in1=bank[0:64, :].rearrange("p (a b) -> p a b", a=8), op=ALU.subtract),
                               reads=[bb, b_ident], writes=[b_Rb[0]])
                            if not fw.pending:
                                yield
                    for k in range(5):
                        Pc, bPc, Pn, bPn = Pb[k % 2], b_Pb[k % 2], Pb[(k + 1) % 2], b_Pb[(k + 1) % 2]
                        Ncur, bNc, Nn, bNn = Nb[k % 2], b_Nb[k % 2], Nb[(k + 1) % 2], b_Nb[(k + 1) % 2]
                        Rc, bRc = Rb[k % 2], b_Rb[k % 2]
                        last = (k == 4)
                        Rn, bRn = (a.TT, a.b_TT) if last else (Rb[(k + 1) % 2], b_Rb[(k + 1) % 2])
                        bank, bb = nbank_d(d)
                        for c in range(8):
                            op(pe, lambda: nc.tensor.matmul(bank[0:64, c * 64:(c + 1) * 64], lhsT=Ncur[:, c, :], rhs=Pc[:, c, :], start=True, stop=True),
                               reads=[bNc, bPc], writes=[bb], inc=(c == 7))
                            if not fw.pending:
                                yield
                        op(act, lambda: nc.scalar.copy(out=Pn[:].rearrange("p a b -> p (a b)"), in_=bank[0:64, :]), reads=[bb], writes=[bPn])
                        if not fw.pending:
                            yield
                        if not fw.pending:
                            yield
                        if not last:
                            bank2, bb2 = nbank_d(d)
                            for c in range(8):
                                op(pe, lambda: nc.tensor.matmul(bank2[0:64, c * 64:(c + 1) * 64], lhsT=Pc[:, c, :], rhs=Ncur[:, c, :], start=True, stop=True),
                                   reads=[bNc, bPc], writes=[bb2], inc=(c == 7))
                                if not fw.pending:
                                    yield
                            op(dve, lambda: nc.vector.tensor_copy(out=Nn[:].rearrange("p a b -> p (a b)"), in_=bank2[0:64, :]), reads=[bb2], writes=[bNn])
                            if not fw.pending:
                                yield
                        bank3, bb3 = nbank_d(d)
                        for c in range(8):
                            op(pe, lambda: nc.tensor.matmul(bank3[0:64, c * 64:(c + 1) * 64], lhsT=ident[0:64, 0:64], rhs=Rc[:, c, :], start=True, stop=False),
                               reads=[b_ident, bRc], writes=[bb3], inc=False)
                            op(pe, lambda: nc.tensor.matmul(bank3[0:64, c * 64:(c + 1) * 64], lhsT=Pn[:, c, :], rhs=Rc[:, c, :], start=False, stop=True),
                               reads=[bPn, bRc], writes=[bb3], inc=(c == 7))
                            if not fw.pending:
                                yield
                        if k % 2 == 0:
                            op(dve, lambda: nc.vector.tensor_copy(out=Rn[:].rearrange("p a b -> p (a b)"), in_=bank3[0:64, :]), reads=[bb3], writes=[bRn])
                        else:
                            op(act, lambda: nc.scalar.copy(out=Rn[:].rearrange("p a b -> p (a b)"), in_=bank3[0:64, :]), reads=[bb3], writes=[bRn])
                        if not fw.pending:
                            yield
                    op(dve, lambda: nc.vector.tensor_tensor(out=a.vb[:], in0=Vtok[:, n0:n0 + G, :, :], in1=bv.unsqueeze(3).to_broadcast([64, G, 2, 128]),
                                                            op=ALU.mult), reads=[b_Vtok, b_g8], writes=[a.b_vb])
                    if not fw.pending:
                        yield
                    op(dve, lambda: nc.vector.tensor_tensor(out=kbg[:], in0=Ktok[:, n0:n0 + G, :].unsqueeze(2).to_broadcast([64, G, 2, 128]),
                                                            in1=smv(2).unsqueeze(3).to_broadcast([64, G, 2, 128]), op=ALU.mult),
                       reads=[b_Ktok, b_sm], writes=[b_kbg])
                    if not fw.pending:
                        yield
                    op(dve, lambda: nc.vector.tensor_tensor(out=a.kgl[:], in0=Ktok[:, n0:n0 + G, :].unsqueeze(2).to_broadcast([64, G, 2, 128]),
                                                            in1=smv(3).unsqueeze(3).to_broadcast([64, G, 2, 128]), op=ALU.mult),
                       reads=[b_Ktok, b_sm], writes=[a.b_kgl])
                    if not fw.pending:
                        yield
                    op(dve, lambda: nc.vector.tensor_tensor(out=a.qgT[:].rearrange("p (n v) c -> p n v c", v=2),
                                                            in0=qT[:, c0:c0 + G * CH].rearrange("p (n c) -> p n c", n=G).unsqueeze(2).to_broadcast([128, G, 2, 64]),
                                                            in1=egr[:].rearrange("p (n v c) -> p n v c", n=G, v=2), op=ALU.mult),
                       reads=[b_qT, b_egr], writes=[a.b_qgT])
                    if not fw.pending:
                        yield
                    bank, bb = nbank_d(d)
                    for c in range(8):
                        op(pe, lambda: nc.tensor.matmul(bank[:, c * 64:(c + 1) * 64], lhsT=kbg[:, c // 2, c % 2, :], rhs=a.TT[:, c, :], start=True, stop=True),
                           reads=[b_kbg, a.b_TT], writes=[bb], inc=(c == 7))
                        if not fw.pending:
                            yield
                    op(act, lambda: nc.scalar.mul(out=a.nwT[:].rearrange("p a b -> p (a b)"), in_=bank[:], mul=-1.0), reads=[bb], writes=[a.b_nwT])
                    if not fw.pending:
                        yield

                def scan(hk, cgs, par):
                    vps_ = lambda ch: banks[6][0:64, ch * 128:(ch + 1) * 128]
                    sps_ = lambda ch: banks[7][:, ch * 128:(ch + 1) * 128]
                    ops_f = lambda d, v, nl: banks[4 + d][:, (v * G + nl) * 64:(v * G + nl + 1) * 64]
                    for s in range(G):
                        chains = []
                        for d in range(2):
                            nl = s if d == 0 else G - 1 - s
                            for v in range(2):
                                chains.append((d, v, nl, d * 2 + v, nl * 2 + v, bas[d][par]))
                        for i_, (d, v, nl, ch, c, a) in enumerate(chains):
                            op(pe, lambda: nc.tensor.matmul(vps_(ch), lhsT=a.TT[:, c, :], rhs=a.vb[:, nl, v, :], start=True, stop=False),
                               reads=[a.b_TT, a.b_vb], writes=[b_bk[6]], inc=False)
                            op(pe, lambda: nc.tensor.matmul(vps_(ch), lhsT=a.nwT[:, c, :], rhs=Sbf[:, ch, :], start=False, stop=True),
                               reads=[a.b_nwT, b_Sbf], writes=[b_bk[6]], inc=(i_ == 3))
                        yield
                        op(act, lambda: nc.scalar.copy(out=vn[:].rearrange("p a b -> p (a b)"), in_=banks[6][0:64, :]), reads=[b_bk[6]], writes=[b_vnall])
                        yield
                        for i_, (d, v, nl, ch, c, a) in enumerate(chains):
                            op(pe, lambda: nc.tensor.matmul(ops_f(d, v, nl), lhsT=Sbf[:, ch, :], rhs=a.qgT[:, c, :], start=True, stop=False),
                               reads=[b_Sbf, a.b_qgT], writes=[b_bk[4 + d]], inc=False)
                            op(pe, lambda: nc.tensor.matmul(ops_f(d, v, nl), lhsT=vn[:, ch, :], rhs=a.qkT[:, c, :], start=False, stop=True),
                               reads=[b_vnall, a.b_qkT], writes=[b_bk[4 + d]], inc=False)
                            op(pe, lambda: nc.tensor.matmul(sps_(ch), lhsT=a.kgl[:, nl, v, :], rhs=vn[:, ch, :], start=True, stop=True),
                               reads=[a.b_kgl, b_vnall], writes=[b_bk[7]], inc=(i_ == 3))
                        yield
                        for (d, v, nl, ch, c, a) in chains:
                            pass
                        op(dve, lambda: nc.vector.tensor_tensor(out=Sst[:], in0=Sst[:], in1=eg4[par][:, s, :].unsqueeze(2).to_broadcast([128, 4, 128]), op=ALU.mult),
                           reads=[b_Sst, b_eg4[par]], writes=[b_Sst])
                        op(dve, lambda: nc.vector.tensor_tensor(out=Sst[:].rearrange("p a b -> p (a b)"), in0=banks[7][:, :], in1=Sst[:].rearrange("p a b -> p (a b)"), op=ALU.add),
                           reads=[b_Sst, b_bk[7]], writes=[b_Sst])
                        yield
                        op(act, lambda: nc.scalar.copy(out=Sbf[:].rearrange("p a b -> p (a b)"), in_=Sst[:].rearrange("p a b -> p (a b)")),
                           reads=[b_Sst], writes=[b_Sbf])
                        yield
                    for d in range(2):
                        c0 = cgs[d] * G * CH
                        for v in range(2):
                            dst = oT[:, v * S + c0:v * S + c0 + G * CH]
                            op(dve, lambda: nc.vector.tensor_tensor(out=dst, in0=banks[4 + d][:, v * G * 64:(v + 1) * G * 64], in1=dst, op=ALU.add),
                               reads=[b_bk[4 + d], b_oT], writes=[b_oT])

                for hk in range(16):
                    def chain_g(hk=hk):
                        fw.dma(pool, gall, gproc[s0:s0 + S, :].rearrange("(n p) c -> p n c", p=64), writes=[b_oT])
                        yield
                        for kind in range(2):
                            for d in range(2):
                                col = d * 64 + kind * 32 + 2 * hk
                                op(dve, lambda: nc.vector.tensor_copy(out=g8[:, kind * 2 + d, :, :], in_=gall[:, :, col:col + 2]), reads=[b_oT], writes=[b_g8])
                                yield
                        op(dve, lambda: nc.vector.memset(oT[:], 0.0), writes=[b_oT])
                        yield
                        op(dve, lambda: nc.vector.memset(Sst[:], 0.0), writes=[b_Sst])
                        yield
                        op(dve, lambda: nc.vector.memset(Sbf[:], 0.0), writes=[b_Sbf])
                        yield
                    rr([chain_q(psets[0], hk), chain_k(psets[1], hk), chain_g()])
                    rr([chain_v(psets[0], hk, 0), chain_v(psets[1], hk, 1)])
                    DN_STOP = 9
                    HK_MAX = 16
                    if hk >= HK_MAX:
                        break
                    def pre_batch(sgi):
                        gens = [precompute(hk, d, [sgi, NG - 1 - sgi][d], sgi % 2) for d in range(2)]
                        for _o in range(OFFS):
                            try:
                                next(gens[0])
                                yield
                            except StopIteration:
                                gens.pop(0)
                                break
                        while gens:
                            for g_ in list(gens):
                                try:
                                    next(g_)
                                    yield
                                except StopIteration:
                                    gens.remove(g_)

                    def drain(g):
                        for _ in g:
                            pass
                    drain(pre_batch(0))
                    KPRE = 5
                    for sgi in range(NG):
                        nxt = pre_batch(sgi + 1) if sgi + 1 < NG else iter(())
                        for _ in scan(hk, [sgi, NG - 1 - sgi], sgi % 2):
                            for _k in range(KPRE):
                                next(nxt, None)
                        drain(nxt)
                    if DN_STOP < 4:
                        continue
                    rr([chain_fin(psets[0], hk, 0), chain_fin(psets[1], hk, 1)])
                fw.barrier()
                fw.release_dsems()

        def fn_core(S, s0):
            tab = tab_of[S]
            NSC = S // 128
            with ExitStack() as ph:
                sbp = mksb(ph)
                yres = sbp("yres", [128, NSC, 1024], BF16); b_yres = Buf("yres")
                NSL = 4
                ring = [sbp(f"fr{i}", [128, 4096], BF16) for i in range(NSL)]; b_ring = [Buf(f"fr{i}") for i in range(NSL)]
                mst = [sbp(f"mst{i}", [128, 4, 512], BF16) for i in range(2)]; b_mst = [Buf(f"mst{i}") for i in range(2)]
                rn_ = [0]
                for ps_ in range(4):
                    fw.dma(pool, yres[:], ycs[s0:s0 + S, ps_ * 1024:(ps_ + 1) * 1024].rearrange("(sc p) c -> p sc c", p=128), writes=[b_yres])
                    for kt in range(S // 512):
                        accs = [(banks[i], b_bk[i]) for i in range(4)] if kt % 2 == 0 else [(banks[4 + i], b_bk[4 + i]) for i in range(4)]
                        for sl in range(NSC // 4):
                            i = rn_[0] % NSL
                            rn_[0] += 1
                            fw.dma(sp, ring[i][:], tab[kt, sl], writes=[b_ring[i]])
                            slot, bs = ring[i], b_ring[i]
                            for s4 in range(4):
                                sc = sl * 4 + s4
                                for cc in range(4):
                                    gg, hf = cc // 2, cc % 2
                                    bank, bb = accs[cc]
                                    for t in range(2):
                                        first = (sc == 0 and t == 0)
                                        lastm = (sc == NSC - 1 and t == 1)
                                        op(pe, lambda: nc.tensor.matmul(bank[:], lhsT=yres[:, sc, gg * 512 + t * 256 + hf * 128: gg * 512 + t * 256 + (hf + 1) * 128],
                                                                        rhs=slot[:, (s4 * 2 + t) * 512:(s4 * 2 + t + 1) * 512], start=first, stop=lastm),
                                           reads=[b_yres, bs], writes=[bb], inc=lastm or (s4 == 3 and cc == 3 and t == 1))
                        ms, bms = mst[kt % 2], b_mst[kt % 2]
                        for cc in range(4):
                            bank, bb = accs[cc]
                            if cc % 2 == 0:
                                op(act, lambda: nc.scalar.copy(out=ms[:, cc, :], in_=bank[:]), reads=[bb], writes=[bms])
                            else:
                                op(dve, lambda: nc.vector.tensor_copy(out=ms[:, cc, :], in_=bank[:]), reads=[bb], writes=[bms])
                        fw.dma(pool, mixT[ps_ * 512:(ps_ + 1) * 512, s0 + kt * 512:s0 + (kt + 1) * 512].rearrange("(c p) t -> p c t", p=128),
                               ms[:], reads=[bms])
                fw.barrier()
                fw.release_dsems()

        def row_phase(body):
            with ExitStack() as ph:
                R = row_ctx(ph)
                for t0 in range(0, NT, T):
                    body(R, t0)
                fw.barrier()
                fw.release_dsems()

        if dbg == "ffn":
            def body(R, t0):
                R.load_x(x_d, t0); R.ffn(0, 0); R.ffn(1, 4); R.ffn(2, 1); R.ffn(3, 5); R.final_norm(); R.store_x(y_d, t0)
            row_phase(body)
        elif dbg == "dnmix":
            def bodyA(R, t0):
                R.load_x(x_d, t0); R.norm_transpose(2); R.dn_inproj(t0)
            row_phase(bodyA)
            for S, s0 in zip(seqs, soff):
                dn_core(S, s0)
            def bodyC(R, t0):
                R.load_x(x_d, t0); R.dn_outproj(t0); R.store_x(y_d, t0)
            row_phase(bodyC)
        elif dbg == "fnmix":
            def bodyA(R, t0):
                R.load_x(x_d, t0); R.norm_transpose(3); R.fn_step1(t0)
            row_phase(bodyA)
            for S, s0 in zip(seqs, soff):
                fn_core(S, s0)
            def bodyC(R, t0):
                R.load_x(x_d, t0); R.fn_outproj(t0); R.store_x(y_d, t0)
            row_phase(bodyC)
        else:
            def bodyA(R, t0):
                R.load_x(x_d, t0); R.ffn(0, 0); R.store_x(xres, t0); R.norm_transpose(2); R.dn_inproj(t0)
                if t0 + T >= NT:
                    for _ in late_gen:
                        pass
            row_phase(bodyA)
            for S, s0 in zip(seqs, soff):
                dn_core(S, s0)
            def bodyC(R, t0):
                R.load_x(xres, t0); R.dn_outproj(t0); R.ffn(1, 4); R.ffn(2, 1); R.store_x(xres, t0); R.norm_transpose(3); R.fn_step1(t0)
            row_phase(bodyC)
            for S, s0 in zip(seqs, soff):
                fn_core(S, s0)
            def bodyE(R, t0):
                R.load_x(xres, t0); R.fn_outproj(t0); R.ffn(3, 5); R.final_norm(); R.store_x(y_d, t0)
            row_phase(bodyE)
        fw.barrier()
    return nc


def _dft_table(S):
    s = np.arange(S, dtype=np.int64)
    ang = (np.outer(s, s) % S).astype(np.float64) * (2.0 * np.pi / S)
    sc = 1.0 / np.sqrt(S)
    tab = np.stack([np.cos(ang) * sc, -np.sin(ang) * sc], axis=0).astype(np.float32)
    tab = tab.reshape(2, S // 512, 4, 128, S // 512, 512)
    tab = np.transpose(tab, (4, 1, 3, 2, 0, 5))
    return np.ascontiguousarray(tab.reshape(S // 512, S // 512, 128, 4096)).astype(ml_dtypes.bfloat16)


def _host_consts(inp, seqs):
    gl = [inp["ffn1_norm"][0], inp["ffn1_norm"][1], inp["mix_norm"][0], inp["mix_norm"][1],
          inp["ffn2_norm"][0], inp["ffn2_norm"][1]]
    c = {}
    c["gainsT"] = np.concatenate([np.ascontiguousarray(g.reshape(NKC, 128).T) for g in gl], axis=1).astype(np.float32)
    c["gfin_bc"] = np.ascontiguousarray(np.broadcast_to(inp["final_norm"][None, :], (128, D))).astype(np.float32)
    c["ident"] = np.eye(128, dtype=np.float32).astype(ml_dtypes.bfloat16)
    cw = inp["dn_conv_w"][0]
    c["convwT"] = np.ascontiguousarray(cw.reshape(5, 64, 128).transpose(2, 1, 0).reshape(128, 320)).astype(np.float32)
    c["alog_bc"] = np.ascontiguousarray(np.broadcast_to(inp["dn_a_log"][0].reshape(1, 64), (128, 64))).astype(np.float32)
    c["dtb_bc"] = np.ascontiguousarray(np.broadcast_to(inp["dn_dt_bias"][0].reshape(1, 64), (128, 64))).astype(np.float32)
    c["onorm"] = np.ascontiguousarray(inp["dn_out_norm"][0].reshape(128, 1)).astype(np.float32)
    i = np.arange(64)[:, None]; j = np.arange(64)[None, :]
    tri0 = (i <= j).astype(np.float32); tri1 = (i >= j).astype(np.float32)
    neg0 = np.where(j <= i, 0.0, -30000.0).astype(np.float32); neg1 = np.where(j >= i, 0.0, -30000.0).astype(np.float32)
    st0 = (j < i).astype(np.float32); st1 = (j > i).astype(np.float32)
    c["cmask"] = np.ascontiguousarray(np.concatenate([tri0, tri1, neg0, neg1, st0, st1], axis=1))
    a = np.arange(256, dtype=np.float64)
    ang = np.outer(a, a) * (2 * np.pi / 256)
    cs = np.concatenate([np.cos(ang), np.sin(ang)], axis=1) / 16.0
    c["cs256"] = np.ascontiguousarray(cs.reshape(2, 128, 512).transpose(1, 0, 2).reshape(128, 1024)).astype(np.float32).astype(ml_dtypes.bfloat16)
    for S in sorted(set(seqs)):
        c[f"dft{S}"] = _dft_table(S)
    return c


W_NAMES = ["ffn1_w_gate", "ffn1_w_up", "ffn1_w_down", "ffn2_w_gate", "ffn2_w_up", "ffn2_w_down",
           "dn_w_in", "dn_w_out", "fn_w_out"]


def kernel(**inp):
    inp = {k: np.asarray(v) for k, v in inp.items()}
    xp, xs = inp["x_prompt"], inp["x_sample"]
    B, S, _ = xp.shape
    B2, S2, _ = xs.shape
    n = 8
    nc = build([S, S2])
    consts = _host_consts(inp, [S, S2])
    in_maps = []
    for c in range(n):
        m = {"x": np.ascontiguousarray(np.concatenate([xp[c], xs[c % B2]], axis=0))}
        for k in W_NAMES:
            m[k] = inp[k]
        m.update(consts)
        in_maps.append(m)
    res = run_bass_kernel_spmd(nc, in_maps, core_ids=list(range(n)))
    yp = np.stack([res.results[c]["y"][:S] for c in range(n)], axis=0)
    ys = np.stack([res.results[c]["y"][S:] for c in range(B2)], axis=0)
    return (yp.astype(np.float32), ys.astype(np.float32))
```
